# Optimizing a Trainium2 kernel written in Bass

```python
import jax
import jax.numpy as jnp
from jax import lax
import numpy as np

D_MODEL = 2048
BATCH = 16
SEQ = 2048
DEPTH = 1

HGRN_HEAD_DIM = 128
HGRN_HEADS = D_MODEL // HGRN_HEAD_DIM
HGRN_WIDTH = HGRN_HEADS * HGRN_HEAD_DIM
HGRN_CHUNK = 32
ATTN_GROUPS = ((128, 1), (512, 4), (2048, 16))
ATTN_HEADS_PER_GROUP = 4
HEAD_DIM = 128
ATTN_QKV_WIDTH = len(ATTN_GROUPS) * 3 * ATTN_HEADS_PER_GROUP * HEAD_DIM
ATTN_OUT_WIDTH = ATTN_HEADS_PER_GROUP * HEAD_DIM
ROPE_THETA = 500000.0
ROPE_DIM = HEAD_DIM // 4
N_BRANCHES = 2
IN_COLS = 5 * HGRN_WIDTH + ATTN_QKV_WIDTH + N_BRANCHES * D_MODEL
D_FF = ((8 * D_MODEL // 3 + 255) // 256) * 256
DEEPNORM_ALPHA = (2.0 * DEPTH) ** 0.25
DEEPNORM_BETA = (8.0 * DEPTH) ** -0.25
LN_EPS = 1e-5
NEG_INF = -1e30

kernel_name = 'hybrid_hgrn2_dilated_attn_macaron_deepnorm'


def layer_norm(x, g, b):
    xf = x.astype(jnp.float32)
    mu = jnp.mean(xf, axis=-1, keepdims=True)
    var = jnp.mean(jnp.square(xf - mu), axis=-1, keepdims=True)
    return ((xf - mu) * lax.rsqrt(var + LN_EPS) * g + b).astype(x.dtype)


def swiglu(x, w_in, w_out):
    gate, up = jnp.split(x @ w_in, 2, axis=-1)
    return (jax.nn.silu(gate) * up) @ w_out


def partial_rope(t, pos):
    t = t.astype(jnp.float32)
    inv_freq = ROPE_THETA ** (-jnp.arange(0, ROPE_DIM, 2, dtype=jnp.float32) / ROPE_DIM)
    ang = pos.astype(jnp.float32)[:, None] * inv_freq
    cos = jnp.cos(ang)[None, :, None, None, :]
    sin = jnp.sin(ang)[None, :, None, None, :]
    t1, t2, rest = jnp.split(t, [ROPE_DIM // 2, ROPE_DIM], axis=-1)
    return jnp.concatenate([t1 * cos - t2 * sin, t2 * cos + t1 * sin, rest], axis=-1)


def dilated_window_attention(q, k, v, window, dilation):
    b_, s_, h_, dh = q.shape
    half = window // (2 * dilation)
    seg = s_ // dilation
    blk = half
    n_blk = -(-seg // blk)
    seg_p = n_blk * blk

    def to_residue(t):
        t = t.reshape(b_, seg, dilation, h_, dh).transpose(0, 2, 3, 1, 4)
        return jnp.pad(t, ((0, 0), (0, 0), (0, 0), (0, seg_p - seg), (0, 0)))

    def neighbours(t):
        t = jnp.pad(t, ((0, 0), (0, 0), (0, 0), (blk, blk), (0, 0)))
        t = t.reshape(b_, dilation, h_, n_blk + 2, blk, dh)
        return jnp.concatenate([t[:, :, :, :-2], t[:, :, :, 1:-1], t[:, :, :, 2:]], axis=4)

    qr = to_residue(q).reshape(b_, dilation, h_, n_blk, blk, dh)
    kr = neighbours(to_residue(k))
    vr = neighbours(to_residue(v))
    qi = jnp.arange(seg_p).reshape(n_blk, blk, 1)
    kj = (jnp.arange(n_blk)[:, None, None] - 1) * blk + jnp.arange(3 * blk)[None, None, :]
    valid = (jnp.abs(qi - kj) <= half) & (kj >= 0) & (kj < seg)
    s = jnp.einsum('brhnqe,brhnke->brhnqk', qr, kr).astype(jnp.float32) * (HEAD_DIM ** -0.5)
    s = jnp.where(valid, s, NEG_INF)
    m = jnp.max(s, axis=-1, keepdims=True)
    p = jnp.exp(s - m)
    denom = jnp.sum(p, axis=-1, keepdims=True)
    o = jnp.einsum('brhnqk,brhnke->brhnqe', p, vr.astype(jnp.float32)) / denom
    lse = (m + jnp.log(denom))[..., 0]
    o = o.reshape(b_, dilation, h_, seg_p, dh)[:, :, :, :seg]
    o = o.transpose(0, 3, 1, 2, 4).reshape(b_, s_, h_, dh)
    lse = lse.reshape(b_, dilation, h_, seg_p)[:, :, :, :seg]
    lse = lse.transpose(0, 3, 1, 2).reshape(b_, s_, h_)
    return o, lse


def dilated_attention_mixer(h_qkv):
    b_, s_, _ = h_qkv.shape
    qkv = h_qkv.reshape(b_, s_, len(ATTN_GROUPS), 3, ATTN_HEADS_PER_GROUP, HEAD_DIM)
    pos = jnp.arange(s_)
    q = partial_rope(qkv[:, :, :, 0], pos)
    k = partial_rope(qkv[:, :, :, 1], pos)
    v = qkv[:, :, :, 2]
    outs, lses = [], []
    for g, (window, dilation) in enumerate(ATTN_GROUPS):
        o_g, lse_g = dilated_window_attention(q[:, :, g], k[:, :, g], v[:, :, g], window, dilation)
        outs.append(o_g)
        lses.append(lse_g)
    w = jax.nn.softmax(jnp.stack(lses, axis=0), axis=0)
    o = jnp.sum(w[..., None] * jnp.stack(outs, axis=0), axis=0)
    return o.reshape(b_, s_, ATTN_OUT_WIDTH)


def hgrn2_chunk_scan(q, f, v):
    b_, s_, h_, dk = q.shape
    dv = v.shape[-1]
    n_chunks = s_ // HGRN_CHUNK

    def chunks(t):
        return t.reshape(b_, n_chunks, HGRN_CHUNK, h_, t.shape[-1]).transpose(0, 3, 1, 2, 4)

    qc, fc, vc = chunks(q), chunks(f), chunks(v)
    kc = 1.0 - fc
    cum = jnp.cumsum(jnp.log(fc), axis=3)
    cum_last = cum[:, :, :, -1:]
    q_dec = qc * jnp.exp(cum)
    k_dec = kc * jnp.exp(-cum)
    k_end = kc * jnp.exp(cum_last - cum)
    tril = jnp.tril(jnp.ones((HGRN_CHUNK, HGRN_CHUNK), dtype=bool))
    a = jnp.where(tril, jnp.einsum('bhncd,bhnsd->bhncs', q_dec, k_dec), 0.0)
    o_intra = jnp.einsum('bhncs,bhnsv->bhncv', a, vc)
    decay = jnp.exp(cum_last[:, :, :, 0])

    def step(state, inp):
        q_n, k_n, v_n, dec_n = inp
        o_n = jnp.einsum('bhcd,bhdv->bhcv', q_n, state)
        state = dec_n[..., None] * state + jnp.einsum('bhcd,bhcv->bhdv', k_n, v_n)
        return state, o_n

    xs = (jnp.moveaxis(q_dec, 2, 0), jnp.moveaxis(k_end, 2, 0),
          jnp.moveaxis(vc, 2, 0), jnp.moveaxis(decay, 2, 0))
    _, o_inter = lax.scan(step, jnp.zeros((b_, h_, dk, dv), jnp.float32), xs)
    o = o_intra + jnp.moveaxis(o_inter, 0, 2)
    return o.transpose(0, 2, 3, 1, 4).reshape(b_, s_, h_, dv)


def bidirectional_hgrn2(hq, hf_fwd, hf_bwd, hi, hog, lb_fwd, lb_bwd, layer, norm_g):
    b_, s_, _ = hq.shape

    def heads(t):
        return t.astype(jnp.float32).reshape(b_, s_, HGRN_HEADS, HGRN_HEAD_DIM)

    def forget(hf, lb_table):
        lb = jnp.cumsum(jax.nn.softmax(lb_table.astype(jnp.float32), axis=0), axis=0)[layer]
        return heads(lb + (1.0 - lb) * jax.nn.sigmoid(hf.astype(jnp.float32)))

    q = heads(jax.nn.silu(hq.astype(jnp.float32)))
    i = heads(hi)
    f_f = forget(hf_fwd, lb_fwd)
    f_b = forget(hf_bwd, lb_bwd)
    rev = lambda t: jnp.flip(t, axis=1)
    o = hgrn2_chunk_scan(q, f_f, i) + rev(hgrn2_chunk_scan(rev(q), rev(f_b), rev(i)))
    o = o * lax.rsqrt(jnp.mean(jnp.square(o), axis=-1, keepdims=True) + LN_EPS)
    return o.reshape(b_, s_, HGRN_WIDTH) * norm_g * jax.nn.silu(hog.astype(jnp.float32))


def hybrid_mixer(h, w_in, lb_fwd, lb_bwd, layer, hgrn_norm_g, w_a, w_b, w_out):
    proj = h @ w_in
    splits = np.cumsum([HGRN_WIDTH] * 5 + [ATTN_QKV_WIDTH]).tolist()
    hq, hf_fwd, hf_bwd, hi, hog, h_qkv, h_gate = jnp.split(proj, splits, axis=-1)
    y_a = bidirectional_hgrn2(hq, hf_fwd, hf_bwd, hi, hog, lb_fwd, lb_bwd, layer,
                              hgrn_norm_g).astype(h.dtype) @ w_a
    y_b = dilated_attention_mixer(h_qkv).astype(h.dtype) @ w_b
    g_a, g_b = jnp.split(jax.nn.sigmoid(h_gate), N_BRANCHES, axis=-1)
    return (g_a * y_a + g_b * y_b) @ w_out


def setup_inputs(seed: int = 0) -> dict:
    key = jax.random.key(seed)
    ks = jax.random.split(key, 18)

    def normal(k, shape):
        return jax.random.normal(k, shape, jnp.float32)

    def dense(k, shape, scale=1.0):
        return normal(k, shape) * (shape[-2] ** -0.5) * scale

    def gain(k, shape):
        return 1.0 + 0.02 * normal(k, shape)

    def bias(k, shape):
        return 0.02 * normal(k, shape)

    return {
        'x': normal(ks[0], (BATCH, SEQ, D_MODEL)),
        'ffn1_w_in': dense(ks[1], (DEPTH, D_MODEL, 2 * D_FF)),
        'ffn1_w_out': dense(ks[2], (DEPTH, D_FF, D_MODEL), DEEPNORM_BETA),
        'ln1_g': gain(ks[3], (DEPTH, D_MODEL)),
        'ln1_b': bias(ks[4], (DEPTH, D_MODEL)),
        'mix_w_in': dense(ks[5], (DEPTH, D_MODEL, IN_COLS)),
        'hgrn_lb_fwd': 0.1 * normal(ks[6], (DEPTH + 1, HGRN_WIDTH)),
        'hgrn_lb_bwd': 0.1 * normal(ks[7], (DEPTH + 1, HGRN_WIDTH)),
        'hgrn_norm_g': gain(ks[8], (DEPTH, HGRN_WIDTH)),
        'w_branch_a': dense(ks[9], (DEPTH, HGRN_WIDTH, D_MODEL), DEEPNORM_BETA),
        'w_branch_b': dense(ks[10], (DEPTH, ATTN_OUT_WIDTH, D_MODEL), DEEPNORM_BETA),
        'mix_w_out': dense(ks[11], (DEPTH, D_MODEL, D_MODEL), DEEPNORM_BETA),
        'ln2_g': gain(ks[12], (DEPTH, D_MODEL)),
        'ln2_b': bias(ks[13], (DEPTH, D_MODEL)),
        'ffn2_w_in': dense(ks[14], (DEPTH, D_MODEL, 2 * D_FF)),
        'ffn2_w_out': dense(ks[15], (DEPTH, D_FF, D_MODEL), DEEPNORM_BETA),
        'ln3_g': gain(ks[16], (DEPTH, D_MODEL)),
        'ln3_b': bias(ks[17], (DEPTH, D_MODEL)),
    }


def reference(x, ffn1_w_in, ffn1_w_out, ln1_g, ln1_b, mix_w_in, hgrn_lb_fwd, hgrn_lb_bwd,
              hgrn_norm_g, w_branch_a, w_branch_b, mix_w_out, ln2_g, ln2_b,
              ffn2_w_in, ffn2_w_out, ln3_g, ln3_b):
    h = x
    for layer in range(DEPTH):
        h = layer_norm(DEEPNORM_ALPHA * h + 0.5 * swiglu(h, ffn1_w_in[layer], ffn1_w_out[layer]),
                       ln1_g[layer], ln1_b[layer])
        mix = hybrid_mixer(h, mix_w_in[layer], hgrn_lb_fwd, hgrn_lb_bwd, layer, hgrn_norm_g[layer],
                           w_branch_a[layer], w_branch_b[layer], mix_w_out[layer])
        h = layer_norm(DEEPNORM_ALPHA * h + mix, ln2_g[layer], ln2_b[layer])
        h = layer_norm(DEEPNORM_ALPHA * h + 0.5 * swiglu(h, ffn2_w_in[layer], ffn2_w_out[layer]),
                       ln3_g[layer], ln3_b[layer])
    return h
```

```python
import contextlib
import numpy as np
import concourse.bass as bass
import concourse.mybir as mybir
from concourse.bass_utils import run_bass_kernel_spmd

F32 = mybir.dt.float32
BF16 = mybir.dt.bfloat16
AF = mybir.ActivationFunctionType
ALU = mybir.AluOpType
AX = mybir.AxisListType

D = 2048
KC = 16
DFF = 5632
NFF = 44
S = 2048
NCORES = 8
ALPHA = 2.0 ** 0.25
EPS = 1e-5
QKV0 = 5 * 2048
GATE0 = QKV0 + 4608
ENGS = ("pe", "act", "dve", "pool", "sp")


class Op:
    __slots__ = ("eng", "fn", "deps", "signal", "ticket", "chan", "idx", "is_dma")

    def __init__(self, eng, fn, is_dma=False):
        self.eng = eng
        self.fn = fn
        self.deps = []
        self.signal = False
        self.ticket = None
        self.chan = None
        self.idx = None
        self.is_dma = is_dma


class Prog:
    def __init__(self, nc):
        self.nc = nc
        self.ops = {e: [] for e in ENGS}
        self.bufs = {}
        self.dma_ops = []
        self.chan_last = {}
        self.bar_deps = []
        self.bar_pending = set()
        self.all_ops = []
        self.chmap = {}

    def add(self, eng, fn, reads=(), writes=(), dma=False, chan=None):
        op = Op(eng, fn, is_dma=dma)
        if dma:
            lk = chan if chan is not None else (writes[0] if writes else reads[0])
            lk = (eng, lk)
            if lk not in self.chmap:
                self.chmap[lk] = (eng, sum(1 for k_ in self.chmap if k_[0] == eng))
            op.chan = self.chmap[lk]
        op.idx = len(self.ops[eng])
        self.all_ops.append(op)
        deps = {}

        def add_dep(d):
            if d is None or d is op:
                return
            k = id(d) if d.is_dma else d.eng
            cur = deps.get(k)
            if cur is None or (not d.is_dma and d.idx > cur.idx):
                deps[k] = d

        for key in reads:
            ent = self.bufs.get(key)
            if ent is not None:
                add_dep(ent[0])
        for key in writes:
            ent = self.bufs.get(key)
            if ent is not None:
                add_dep(ent[0])
                for r in ent[1].values():
                    add_dep(r)
        if eng in self.bar_pending:
            self.bar_pending.discard(eng)
            for d in self.bar_deps:
                add_dep(d)
        for d in deps.values():
            if (not d.is_dma) and (not dma) and d.eng == eng and eng == "pe":
                continue
            op.deps.append(d)
            d.signal = True
        for key in reads:
            ent = self.bufs.setdefault(key, [None, {}])
            ent[1][id(op) if dma else eng] = op
        for key in writes:
            self.bufs[key] = [op, {}]
        self.ops[eng].append(op)
        if dma:
            self.dma_ops.append(op)
            self.chan_last[op.chan] = op
        return op

    def barrier(self):
        deps = []
        for e in ENGS:
            for op in reversed(self.ops[e]):
                if not op.is_dma:
                    deps.append(op)
                    break
        deps.extend(self.chan_last.values())
        self.bar_deps = deps
        self.bar_pending = set(ENGS)
        self.bufs = {}
        self.chmap = {}

    def emit(self):
        nc = self.nc
        with contextlib.ExitStack() as es:
            esem = {e: es.enter_context(nc.semaphore("es_" + e)) for e in ENGS}
            chan_sems = {}
            chan_cnt = {}
            for d in self.dma_ops:
                if d.chan not in chan_sems:
                    chan_sems[d.chan] = es.enter_context(nc.semaphore("dc_%d" % len(chan_sems)))
                    chan_cnt[d.chan] = 0
            ecnt = {e: 0 for e in ENGS}
            for op in self.all_ops:
                if op.is_dma:
                    chan_cnt[op.chan] += 16
                    op.ticket = (chan_sems[op.chan], chan_cnt[op.chan])
                elif op.signal:
                    ecnt[op.eng] += 1
                    op.ticket = (esem[op.eng], ecnt[op.eng])
            finals = list(self.chan_last.values())
            block = es.enter_context(nc.Block())

            def run(e, eng):
                waited = {}
                for op in self.ops[e]:
                    for d in op.deps:
                        sem, val = d.ticket
                        k = id(sem)
                        if waited.get(k, 0) >= val:
                            continue
                        eng.wait_ge(sem, val)
                        waited[k] = val
                    ins = op.fn(eng)
                    if op.is_dma:
                        ins.then_inc(op.ticket[0], 16)
                    elif op.signal:
                        ins.then_inc(op.ticket[0], 1)
                if e == "sp":
                    for d in finals:
                        sem, val = d.ticket
                        if waited.get(id(sem), 0) >= val:
                            continue
                        eng.wait_ge(sem, val)
                        waited[id(sem)] = val

            @block.tensor
            def _(eng):
                run("pe", eng)

            @block.scalar
            def _(eng):
                run("act", eng)

            @block.vector
            def _(eng):
                run("dve", eng)

            @block.gpsimd
            def _(eng):
                run("pool", eng)

            @block.sync
            def _(eng):
                run("sp", eng)


class K:
    def __init__(self, nseq, debug=False, upto=99, stage=0):
        self.stage = stage
        self.nseq = nseq
        self.ntok = nseq * S
        self.debug = debug
        self.upto = upto
        nc = bass.Bass("TRN2", target_bir_lowering=False)
        self.nc = nc
        self.P = Prog(nc)
        NT = self.ntok
        inp = lambda n, shp: nc.dram_tensor(n, shp, F32, kind="ExternalInput").ap()
        self.x = inp("x", [NT, D])
        self.w = {
            "ffn1_w_in": inp("ffn1_w_in", [D, 2 * DFF]), "ffn1_w_out": inp("ffn1_w_out", [DFF, D]),
            "ffn2_w_in": inp("ffn2_w_in", [D, 2 * DFF]), "ffn2_w_out": inp("ffn2_w_out", [DFF, D]),
            "mix_w_in": inp("mix_w_in", [D, 18944]), "w_a": inp("w_a", [D, D]),
            "w_b": inp("w_b", [512, D]), "w_out": inp("w_out", [D, D]),
        }
        self.vec = {n: inp(n, [1, D]) for n in ("ln1_g", "ln1_b", "ln2_g", "ln2_b", "ln3_g", "ln3_b")}
        self.hvec = inp("hvec", [128, 5 * 16])
        self.identd = inp("ident", [128, 128])
        self.cmaskd = inp("cmask", [1, 512])
        self.trid = inp("tri", [2, 128, 512])
        self.ropd = inp("rope", [2, 128, S])
        self.permd = inp("perm", [128, 128])
        self.amaskd = inp("amask", [23, 128, 512])
        kind = "ExternalOutput" if (debug or stage == 1) else ("ExternalInput" if stage == 2 else "Internal")
        self.h1s = nc.dram_tensor("h1s", [NT, D], F32, kind=kind).ap()
        self.h2s = nc.dram_tensor("h2s", [NT, D], F32, kind="ExternalOutput" if debug else "Internal").ap()
        self.h1Ts = nc.dram_tensor("h1Ts", [128, KC, NT], BF16, kind=kind).ap()
        self.ogs = nc.dram_tensor("ogs", [128, KC, NT], BF16, kind=kind).ap()
        self.ats = nc.dram_tensor("ats", [128, 4, NT], BF16, kind=kind).ap()
        self.y = nc.dram_tensor("y", [NT, D], F32, kind="ExternalOutput").ap()

    def build(self):
        nc, P = self.nc, self.P
        with contextlib.ExitStack() as es:
            sb = lambda n, shp, dt: es.enter_context(nc.sbuf_tensor(n, shp, dt))
            self.ps = [es.enter_context(nc.psum_tensor("ps%d" % i, [128, 512], F32)) for i in range(8)]
            self.ident = sb("ident_sb", [128, 128], F32)
            self.identb = sb("identb_sb", [128, 128], BF16)
            P.add("sp", lambda e: e.dma_start(out=self.ident[:], in_=self.identd), writes=["ident"], dma=True)
            P.add("pool", lambda e: e.dma_start(out=self.identb[:], in_=self.identd), writes=["identb"], dma=True)
            self.wrr = 0
            P.barrier()
            for s in range(self.nseq):
                self.ffn_phase(s, 1)
                P.barrier()
                if self.upto <= 1:
                    continue
                self.hgrn_phase(s)
                P.barrier()
                if self.upto <= 2:
                    continue
                self.attn_phase(s)
                P.barrier()
                if self.upto <= 3:
                    continue
                self.mixout_phase(s)
                P.barrier()
                self.ffn_phase(s, 2)
                P.barrier()
            P.emit()
        return nc

    def wslot(self):
        i = self.wrr
        self.wrr = (self.wrr + 1) % len(self.wsl)
        return i

    def load_w(self, src_ap, view_fn, slot=None):
        P = self.P
        if slot is None:
            slot = self.wslot()
        t = self.wsl[slot]
        dst = view_fn(t)
        P.add("pool", lambda e: e.dma_start(out=dst, in_=src_ap), writes=[("w", slot)], dma=True,
              chan=("w", slot))
        return slot, dst

    def ffn_phase(self, s, which):
        nc, P, ps = self.nc, self.P, self.ps
        if which == 1:
            src, dst = self.x, self.h1s
            w_in, w_out = self.w["ffn1_w_in"], self.w["ffn1_w_out"]
            g_d, b_d = self.vec["ln1_g"], self.vec["ln1_b"]
        else:
            src, dst = self.h2s, self.y
            w_in, w_out = self.w["ffn2_w_in"], self.w["ffn2_w_out"]
            g_d, b_d = self.vec["ln3_g"], self.vec["ln3_b"]
        with contextlib.ExitStack() as es:
            self.uid = getattr(self, "uid", 0) + 1
            sb = lambda n, shp, dt, u=self.uid: es.enter_context(nc.sbuf_tensor("%s_u%d" % (n, u), shp, dt))
            self.wsl = [sb("wsl%d" % i, [128, 8192], BF16) for i in range(4)]
            xT = sb("f_xT", [128, KC, 512], BF16)
            hid = sb("f_hid", [128, NFF, 512], BF16)
            xa = sb("f_xa", [128, 4, D], F32)
            xin = sb("f_xin", [128, D], F32)
            gbc = sb("f_gbc", [128, D], F32)
            bbc = sb("f_bbc", [128, D], F32)
            sg = [sb("f_sg%d" % i, [128, 512], F32) for i in range(2)]
            st = sb("f_st", [128, 4, 4 * 6], F32)
            mv = sb("f_mv", [128, 4, 2], F32)
            rstd = sb("f_rstd", [128, 4], F32)
            hT = sb("f_hT", [128, KC, 512], BF16) if which == 1 else None
            P.add("sp", lambda e: e.dma_start(out=gbc[:], in_=g_d.partition_broadcast(128)), writes=["gbc"], dma=True)
            P.add("sp", lambda e: e.dma_start(out=bbc[:], in_=b_d.partition_broadcast(128)), writes=["bbc"], dma=True)
            xT_keys = [("xT", q, tc) for q in range(4) for tc in range(4)]

            def load_xT(tt):
                t0 = s * S + tt * 512
                for tc in range(4):
                    P.add("sp", lambda e, tc=tc: e.dma_start(out=xin[:], in_=src[t0 + tc * 128: t0 + (tc + 1) * 128, :]),
                          writes=["xin"], dma=True)
                    for q in range(4):
                        b = q
                        for j in range(4):
                            kc = q * 4 + j
                            P.add("pe", lambda e, b=b, j=j, kc=kc: e.transpose(
                                out=ps[b][:, j * 128:(j + 1) * 128], in_=xin[:, kc * 128:(kc + 1) * 128], identity=self.ident[:]),
                                reads=["xin", "ident"], writes=[("ps", b)])
                        P.add("act", lambda e, b=b, q=q, tc=tc: e.activation(
                            out=xT[:, q * 4:(q + 1) * 4, tc * 128:(tc + 1) * 128],
                            in_=ps[b][:].rearrange("p (j t) -> p j t", j=4), func=AF.Copy),
                            reads=[("ps", b)], writes=[("xT", q, tc)])

            def featmajor_out(tt):
                t0 = s * S + tt * 512
                for tc in range(4):
                    self.to_featmajor(xa[:, tc, :], ("xa", tc), hT, "hT", tc)
                P.add("sp", lambda e: e.dma_start(out=self.h1Ts[:, :, t0:t0 + 512], in_=hT[:]),
                      reads=[("hT", q, tc) for q in range(4) for tc in range(4)], writes=[("h1Ts", t0)], dma=True, chan="hT")

            load_xT(0)
            pending_fm = None
            for tt in range(4):
                t0 = s * S + tt * 512
                for jb in range(11):
                    sg_, wg = self.load_w(w_in[:, jb * 512:(jb + 1) * 512].rearrange("(kc p) n -> p kc n", p=128),
                                          lambda t: t[:].rearrange("p (kc n) -> p kc n", kc=KC))
                    su_, wu = self.load_w(w_in[:, DFF + jb * 512: DFF + (jb + 1) * 512].rearrange("(kc p) n -> p kc n", p=128),
                                          lambda t: t[:].rearrange("p (kc n) -> p kc n", kc=KC))
                    for jj in range(4):
                        j = jb * 4 + jj
                        bg, bu = (j % 2) * 2, (j % 2) * 2 + 1
                        for kc in range(KC):
                            P.add("pe", lambda e, bg=bg, wg=wg, kc=kc, jj=jj: e.matmul(
                                ps[bg][:], lhsT=wg[:, kc, jj * 128:(jj + 1) * 128], rhs=xT[:, kc, :],
                                start=(kc == 0), stop=(kc == KC - 1)),
                                reads=[("w", sg_)] + xT_keys, writes=[("ps", bg)])
                        for kc in range(KC):
                            P.add("pe", lambda e, bu=bu, wu=wu, kc=kc, jj=jj: e.matmul(
                                ps[bu][:], lhsT=wu[:, kc, jj * 128:(jj + 1) * 128], rhs=xT[:, kc, :],
                                start=(kc == 0), stop=(kc == KC - 1)),
                                reads=[("w", su_)] + xT_keys, writes=[("ps", bu)])
                        P.add("act", lambda e, bg=bg, j=j: e.activation(out=sg[j % 2][:], in_=ps[bg][:], func=AF.Silu),
                              reads=[("ps", bg)], writes=[("sg", j % 2)])
                        P.add("dve", lambda e, bu=bu, j=j: e.tensor_tensor(out=hid[:, j, :], in0=sg[j % 2][:], in1=ps[bu][:], op=ALU.mult),
                              reads=[("ps", bu), ("sg", j % 2)], writes=[("hid", j)])
                    if jb == 1 and pending_fm is not None:
                        featmajor_out(pending_fm)
                        pending_fm = None
                    if jb == 6:
                        for tc in range(4):
                            P.add("sp", lambda e, tc=tc, t0=t0: e.dma_start(out=xa[:, tc, :], in_=src[t0 + tc * 128: t0 + (tc + 1) * 128, :]),
                                  writes=[("xa", tc)], dma=True)
                            P.add("act", lambda e, tc=tc: e.activation(out=xa[:, tc, :], in_=xa[:, tc, :], func=AF.Copy, scale=float(ALPHA)),
                                  reads=[("xa", tc)], writes=[("xa", tc)])
                for cg in range(4):
                    base = 4 if cg % 2 == 0 else 0
                    for jq in range(4):
                        sw_, wo = self.load_w(
                            w_out[jq * 1408:(jq + 1) * 1408, cg * 512:(cg + 1) * 512].rearrange("(c p) n -> p c n", p=128),
                            lambda t: t[:, 0:11 * 512].rearrange("p (c n) -> p c n", c=11))
                        for c in range(11):
                            j = jq * 11 + c
                            for tc in range(4):
                                P.add("pe", lambda e, tc=tc, j=j, c=c, wo=wo, base=base: e.matmul(
                                    ps[base + tc][:], lhsT=hid[:, j, tc * 128:(tc + 1) * 128], rhs=wo[:, c, :],
                                    start=(j == 0), stop=(j == NFF - 1)),
                                    reads=[("w", sw_), ("hid", j)], writes=[("ps", base + tc)])
                    for tc in range(4):
                        P.add("dve", lambda e, tc=tc, cg=cg, base=base: e.scalar_tensor_tensor(
                            out=xa[:, tc, cg * 512:(cg + 1) * 512], in0=ps[base + tc][:], scalar=0.5,
                            in1=xa[:, tc, cg * 512:(cg + 1) * 512], op0=ALU.mult, op1=ALU.add),
                            reads=[("ps", base + tc), ("xa", tc)], writes=[("xa", tc)])
                if tt < 3:
                    load_xT(tt + 1)
                for tc in range(4):
                    self.layernorm(xa[:, tc, :], ("xa", tc), st[:, tc, :], mv[:, tc, :], rstd[:, tc:tc + 1], gbc, bbc, "gbc", "bbc", tc)
                    P.add("sp", lambda e, tc=tc, t0=t0: e.dma_start(out=dst[t0 + tc * 128: t0 + (tc + 1) * 128, :], in_=xa[:, tc, :]),
                          reads=[("xa", tc)], writes=[("dst", which, t0, tc)], dma=True, chan=("xa", tc))
                if which == 1:
                    pending_fm = tt
            if pending_fm is not None:
                featmajor_out(pending_fm)

    def layernorm(self, z, zkey, st, mv, rstd, gbc, bbc, gk, bk, idx):
        P = self.P
        for c in range(4):
            P.add("dve", lambda e, c=c: e.bn_stats(out=st[:, c * 6:(c + 1) * 6], in_=z[:, c * 512:(c + 1) * 512]),
                  reads=[zkey], writes=[("st", idx, c)])
        P.add("dve", lambda e: e.bn_aggr(out=mv, in_=st), reads=[("st", idx, c) for c in range(4)], writes=[("mv", idx)])
        P.add("dve", lambda e: e.tensor_scalar(out=rstd, in0=mv[:, 1:2], scalar1=float(EPS), scalar2=None, op0=ALU.add),
              reads=[("mv", idx)], writes=[("rstd", idx)])
        P.add("act", lambda e: e.activation(out=rstd, in_=rstd, func=AF.Sqrt), reads=[("rstd", idx)], writes=[("rstd", idx)])
        P.add("dve", lambda e: e.reciprocal(out=rstd, in_=rstd), reads=[("rstd", idx)], writes=[("rstd", idx)])
        P.add("dve", lambda e: e.tensor_scalar(out=z, in0=z, scalar1=mv[:, 0:1], scalar2=rstd,
                                               op0=ALU.subtract, op1=ALU.mult),
              reads=[zkey, ("mv", idx), ("rstd", idx)], writes=[zkey])
        P.add("dve", lambda e: e.tensor_tensor(out=z, in0=z, in1=gbc[:], op=ALU.mult), reads=[zkey, gk], writes=[zkey])
        P.add("dve", lambda e: e.tensor_tensor(out=z, in0=z, in1=bbc[:], op=ALU.add), reads=[zkey, bk], writes=[zkey])

    def to_featmajor(self, z, zkey, hT, hkey, tc):
        P, ps = self.P, self.ps
        for q in range(4):
            b = q % 4
            for j in range(4):
                kc = q * 4 + j
                P.add("pe", lambda e, b=b, j=j, kc=kc: e.transpose(
                    out=ps[b][:, j * 128:(j + 1) * 128], in_=z[:, kc * 128:(kc + 1) * 128], identity=self.ident[:]),
                    reads=[zkey, "ident"], writes=[("ps", b)])
            if q % 2 == 0:
                fn = lambda e, b=b, q=q: e.activation(out=hT[:, q * 4:(q + 1) * 4, tc * 128:(tc + 1) * 128],
                                                      in_=ps[b][:].rearrange("p (j t) -> p j t", j=4), func=AF.Copy)
                eng = "act"
            else:
                fn = lambda e, b=b, q=q: e.tensor_copy(out=hT[:, q * 4:(q + 1) * 4, tc * 128:(tc + 1) * 128],
                                                       in_=ps[b][:].rearrange("p (j t) -> p j t", j=4))
                eng = "dve"
            P.add(eng, fn, reads=[("ps", b)], writes=[(hkey, q, tc)])

    def hgrn_phase(self, s):
        nc, P, ps = self.nc, self.P, self.ps
        w_in = self.w["mix_w_in"]
        with contextlib.ExitStack() as es:
            self.uid = getattr(self, "uid", 0) + 1
            sb = lambda n, shp, dt, u=self.uid: es.enter_context(nc.sbuf_tensor("%s_u%d" % (n, u), shp, dt))
            hT = [sb("g_hT%d" % i, [128, KC, 512], BF16) for i in range(2)]
            hw = [sb("g_hw%d" % i, [128, 5, KC, 128], BF16) for i in range(2)]
            NTMP = 13
            tmp = [[sb("g_t%d_%d" % (i, j), [128, 512], F32) for j in range(NTMP)] for i in range(1)]
            qdT = [sb("g_qdT%d" % i, [128, S], BF16) for i in range(2)]
            kdT = [sb("g_kdT%d" % i, [128, S], BF16) for i in range(2)]
            keT = [sb("g_keT%d" % i, [128, S], BF16) for i in range(2)]
            vT = sb("g_vT", [128, S], BF16)
            hogs = sb("g_hog", [128, S], BF16)
            ke = [sb("g_ke%d" % i, [128, 16, 128], BF16) for i in range(2)]
            v = sb("g_v", [128, 16, 128], BF16)
            Sf = sb("g_Sf", [128, 16, 128], BF16)
            S32 = [sb("g_S32_%d" % i, [128, 128], F32) for i in range(2)]
            dec = [sb("g_dec%d" % i, [128, 16], F32) for i in range(2)]
            Am = [[sb("g_Am%d_%d" % (i, j), [128, 512], BF16) for j in range(2)] for i in range(4)]
            Sbw = sb("g_Sbw", [128, 16, 128], BF16)
            tri = [sb("g_tri%d" % i, [128, 512], BF16) for i in range(2)]
            cmask = sb("g_cmask", [128, 512], F32)
            hv = sb("g_hv", [128, 80], F32)
            lb = [sb("g_lb%d" % i, [128, 16], F32) for i in range(2)]
            oml = [sb("g_oml%d" % i, [128, 16], F32) for i in range(2)]
            ones = sb("g_ones", [128, 128], F32)
            o32 = [sb("g_o32_%d" % i, [128, 512], F32) for i in range(2)]
            sq = [sb("g_sq_%d" % i, [128, 512], F32) for i in range(2)]
            rs = [sb("g_rs_%d" % i, [128, 512], F32) for i in range(2)]
            ogT = [sb("g_ogT%d" % i, [128, S], BF16) for i in range(2)]

            P.add("sp", lambda e: e.dma_start(out=cmask[:], in_=self.cmaskd.partition_broadcast(128)), writes=["cmask"], dma=True)
            P.add("sp", lambda e: e.dma_start(out=hv[:], in_=self.hvec), writes=["hv"], dma=True)
            for i in range(2):
                P.add("pool", lambda e, i=i: e.dma_start(out=tri[i][:], in_=self.trid[i]), writes=[("tri", i)], dma=True)
                P.add("dve", lambda e, i=i: e.tensor_tensor(out=lb[i][:], in0=hv[:, 32 * i:32 * i + 16], in1=hv[:, 32 * i + 16:32 * i + 32],
                                                            op=ALU.subtract), reads=["hv"], writes=[("lb", i)])
                P.add("act", lambda e, i=i: e.activation(out=lb[i][:], in_=lb[i][:], func=AF.Sigmoid), reads=[("lb", i)], writes=[("lb", i)])
                P.add("dve", lambda e, i=i: e.tensor_scalar(out=oml[i][:], in0=lb[i][:], scalar1=-1.0, scalar2=1.0, op0=ALU.mult, op1=ALU.add),
                      reads=[("lb", i)], writes=[("oml", i)])
            P.add("pool", lambda e: e.memset(ones[:], 1.0), writes=["ones"])

            def bcl(ap2):
                return ap2.rearrange("p (c t) -> p c t", c=4)[:, :, 127:128].to_broadcast([128, 4, 128])

            def v3(ap2):
                return ap2.rearrange("p (c t) -> p c t", c=4)

            for hh in range(16):
                hs = hh % 2
                for blk in range(5):
                    c0 = blk * 2048 + hh * 128
                    P.add("pool", lambda e, blk=blk, c0=c0, hs=hs: e.dma_start(
                        out=hw[hs][:, blk, :, :], in_=w_in[:, c0:c0 + 128].rearrange("(kc p) n -> p kc n", p=128)),
                        writes=[("hw", hs, blk)], dma=True, chan=("hw", hs, blk))
                def projA(tt, hs=hs):
                    sl = tt % 2
                    P.add("sp", lambda e, sl=sl, tt=tt: e.dma_start(out=hT[sl][:], in_=self.h1Ts[:, :, s * S + tt * 512: s * S + (tt + 1) * 512]),
                          writes=[("hT", sl)], dma=True, chan=("hT", sl))
                    for blk in range(5):
                        for kc in range(KC):
                            P.add("pe", lambda e, blk=blk, kc=kc, sl=sl, hs=hs: e.matmul(
                                ps[blk][:], lhsT=hw[hs][:, blk, kc, :], rhs=hT[sl][:, kc, :], start=(kc == 0), stop=(kc == KC - 1)),
                                reads=[("hw", hs, blk), ("hT", sl)], writes=[("ps", blk)])

                def evacB(tt):
                    T = tmp[0]
                    tsl = slice(tt * 512, (tt + 1) * 512)
                    P.add("act", lambda e, T=T: e.activation(out=T[0][:], in_=ps[0][:], func=AF.Silu), reads=[("ps", 0)], writes=[("tmp", "q")])
                    P.add("act", lambda e, tsl=tsl: e.activation(out=hogs[:, tsl], in_=ps[4][:], func=AF.Silu), reads=[("ps", 4)], writes=[("hogs", tt)])
                    P.add("act", lambda e, T=T: e.activation(out=T[1][:], in_=ps[1][:], func=AF.Sigmoid), reads=[("ps", 1)], writes=[("tmp", 0, "F")])
                    P.add("act", lambda e, T=T: e.activation(out=T[7][:], in_=ps[2][:], func=AF.Sigmoid), reads=[("ps", 2)], writes=[("tmp", 1, "F")])
                    P.add("act", lambda e, tsl=tsl: e.activation(out=vT[:, tsl], in_=ps[3][:], func=AF.Copy), reads=[("ps", 3)], writes=[("vT", tt)])

                def chainC(tt, hh=hh):
                    sl = 0
                    T = tmp[0]
                    tsl = slice(tt * 512, (tt + 1) * 512)
                    lists = []
                    for d in range(2):
                        ops = []
                        o = 8 * d
                        F, LG, KK, CU, EE, AA = (T[1], T[2], T[3], T[4], T[5], T[6]) if d == 0 else (T[7], T[8], T[9], T[10], T[11], T[12])
                        kF, kLG, kKK, kCU, kEE, kAA = [("tmp", d, nm) for nm in ("F", "LG", "KK", "CU", "EE", "AA")]
                        kq = ("tmp", "q")
                        ops.append(("dve", lambda e, F=F, d=d, hh=hh: e.tensor_scalar(out=F[:], in0=F[:], scalar1=oml[d][:, hh:hh + 1], scalar2=lb[d][:, hh:hh + 1],
                                                                                   op0=ALU.mult, op1=ALU.add), [kF, ("oml", d), ("lb", d)], [kF]))
                        ops.append(("act", lambda e, LG=LG, F=F: e.activation(out=LG[:], in_=F[:], func=AF.Ln), [kF], [kLG]))
                        ops.append(("dve", lambda e, KK=KK, F=F: e.tensor_scalar(out=KK[:], in0=F[:], scalar1=-1.0, scalar2=1.0, op0=ALU.mult, op1=ALU.add), [kF], [kKK]))
                        ops.append(("dve", lambda e, CU=CU, LG=LG: e.tensor_tensor_scan(out=CU[:], data0=cmask[:], data1=LG[:], initial=0.0, op0=ALU.mult, op1=ALU.add),
                                    [kLG, "cmask"], [kCU]))
                        ops.append(("act", lambda e, CU=CU, d=d, tt=tt: e.activation(out=dec[d][:, tt * 4:(tt + 1) * 4], in_=v3(CU[:])[:, :, 127], func=AF.Exp),
                                    [kCU], [("dec", d, tt)]))
                        if d == 0:
                            QE, kQE = CU, kCU
                        else:
                            ops.append(("dve", lambda e, AA=AA, CU=CU: e.tensor_tensor(out=v3(AA[:]), in0=bcl(CU[:]), in1=v3(CU[:]), op=ALU.subtract), [kCU], [kAA]))
                            ops.append(("dve", lambda e, AA=AA, LG=LG: e.tensor_tensor(out=AA[:], in0=AA[:], in1=LG[:], op=ALU.add), [kAA, kLG], [kAA]))
                            QE, kQE = AA, kAA
                        ops.append(("act", lambda e, EE=EE, QE=QE: e.activation(out=EE[:], in_=QE[:], func=AF.Exp), [kQE], [kEE]))
                        ops.append(("dve", lambda e, EE=EE, d=d, tsl=tsl: e.tensor_tensor(out=qdT[d][:, tsl], in0=T[0][:], in1=EE[:], op=ALU.mult), [kq, kEE], [("qdT", d, tt)]))
                        ops.append(("act", lambda e, EE=EE, QE=QE: e.activation(out=EE[:], in_=QE[:], func=AF.Exp, scale=-1.0), [kQE], [kEE]))
                        ops.append(("dve", lambda e, EE=EE, KK=KK, d=d, tsl=tsl: e.tensor_tensor(out=kdT[d][:, tsl], in0=KK[:], in1=EE[:], op=ALU.mult), [kKK, kEE], [("kdT", d, tt)]))
                        if d == 0:
                            ops.append(("dve", lambda e, AA=AA, CU=CU: e.tensor_tensor(out=v3(AA[:]), in0=bcl(CU[:]), in1=v3(CU[:]), op=ALU.subtract), [kCU], [kAA]))
                        else:
                            ops.append(("dve", lambda e, AA=AA, CU=CU, LG=LG: e.tensor_tensor(out=AA[:], in0=CU[:], in1=LG[:], op=ALU.subtract), [kCU, kLG, kEE], [kAA]))
                        ops.append(("act", lambda e, EE=EE, AA=AA: e.activation(out=EE[:], in_=AA[:], func=AF.Exp), [kAA], [kEE]))
                        ops.append(("dve", lambda e, EE=EE, KK=KK, d=d, tsl=tsl: e.tensor_tensor(out=keT[d][:, tsl], in0=KK[:], in1=EE[:], op=ALU.mult), [kKK, kEE], [("keT", d, tt)]))
                        lists.append(ops)
                    n = max(len(l) for l in lists)
                    for i in range(n):
                        for l in lists:
                            if i < len(l):
                                eng, fn, rd, wr = l[i]
                                P.add(eng, fn, reads=rd, writes=wr)

                def transD(tt):
                    for i3, (srcT, skey, dstt, dkey) in enumerate(((keT[0], ("keT", 0, tt), ke[0], ("ke", 0, tt)),
                                                                  (keT[1], ("keT", 1, tt), ke[1], ("ke", 1, tt)),
                                                                  (vT, ("vT", tt), v, ("v", tt)))):
                        bnk = 5 + (i3 % 2)
                        pb = ps[bnk][:].bitcast(BF16)
                        for c4 in range(4):
                            c = tt * 4 + c4
                            P.add("pe", lambda e, pb=pb, c4=c4, c=c, srcT=srcT: e.transpose(
                                out=pb[:, c4 * 128:(c4 + 1) * 128], in_=srcT[:, c * 128:(c + 1) * 128], identity=self.identb[:]),
                                reads=[skey, "identb"], writes=[("ps", bnk)])
                        P.add("act", lambda e, pb=pb, dstt=dstt, tt=tt: e.activation(out=dstt[:, tt * 4:(tt + 1) * 4, :],
                                                                                   in_=pb[:, 0:512].rearrange("p (c t) -> p c t", c=4), func=AF.Copy),
                              reads=[("ps", bnk)], writes=[dkey])

                for tt_ in range(4):
                    projA(tt_); evacB(tt_); chainC(tt_); transD(tt_)
                for tt in range(4):
                    for d in range(2):
                        bnk = (tt % 2) * 2 + d
                        for c4 in range(4):
                            c = tt * 4 + c4
                            P.add("pe", lambda e, d=d, c=c, c4=c4, bnk=bnk: e.matmul(
                                ps[bnk][:, c4 * 128:(c4 + 1) * 128], lhsT=kdT[d][:, c * 128:(c + 1) * 128], rhs=qdT[d][:, c * 128:(c + 1) * 128],
                                start=True, stop=True), reads=[("kdT", d, tt), ("qdT", d, tt)], writes=[("ps", bnk)])
                        P.add("dve", lambda e, d=d, tt=tt, bnk=bnk: e.tensor_tensor(out=Am[tt][d][:], in0=ps[bnk][:], in1=tri[d][:], op=ALU.mult),
                              reads=[("ps", bnk), ("tri", d)], writes=[("Am", tt, d)])
                P.add("dve", lambda e: e.memset(S32[0][:], 0.0), writes=[("S32", 0)])
                P.add("dve", lambda e: e.memset(S32[1][:], 0.0), writes=[("S32", 1)])
                P.add("dve", lambda e: e.memset(Sf[:, 0, :], 0.0), writes=[("Sf", 0)])
                P.add("dve", lambda e: e.memset(Sbw[:, 15, :], 0.0), writes=[("Sbw", 15)])
                for i in range(15):
                    for d in range(2):
                        c = i if d == 0 else 15 - i
                        r = d * 2 + (i % 2)
                        P.add("pe", lambda e, c=c, d=d, r=r: e.matmul(ps[r][:, 0:128], lhsT=ke[d][:, c, :], rhs=v[:, c, :], start=True, stop=True),
                              reads=[("ke", d, c // 4), ("v", c // 4)], writes=[("ps", r)])
                        P.add("dve", lambda e, c=c, d=d, r=r: e.scalar_tensor_tensor(out=S32[d][:], in0=S32[d][:], scalar=dec[d][:, c:c + 1], in1=ps[r][:, 0:128],
                                                                                     op0=ALU.mult, op1=ALU.add),
                              reads=[("ps", r), ("S32", d), ("dec", d, c // 4)], writes=[("S32", d)])
                        if d == 0:
                            P.add("act", lambda e, c=c: e.activation(out=Sf[:, c + 1, :], in_=S32[0][:], func=AF.Copy), reads=[("S32", 0)], writes=[("Sf", c + 1)])
                        else:
                            P.add("act", lambda e, c=c: e.activation(out=Sbw[:, c - 1, :], in_=S32[1][:], func=AF.Copy), reads=[("S32", 1)], writes=[("Sbw", c - 1)])
                for tt in range(4):
                    sl = tt % 2
                    tsl = slice(tt * 512, (tt + 1) * 512)
                    ob = 4 + sl
                    for c4 in range(4):
                        c = tt * 4 + c4
                        cs = slice(c4 * 128, (c4 + 1) * 128)
                        gs = slice(c * 128, (c + 1) * 128)
                        rk = [("v", tt), ("Am", tt, 0), ("Am", tt, 1), ("Sf", c), ("Sbw", c), ("qdT", 0, tt), ("qdT", 1, tt)]
                        P.add("pe", lambda e, ob=ob, cs=cs, c=c, tt=tt: e.matmul(ps[ob][:, cs], lhsT=v[:, c, :], rhs=Am[tt][0][:, cs], start=True, stop=False),
                              reads=rk, writes=[("ps", ob)])
                        P.add("pe", lambda e, ob=ob, cs=cs, c=c, tt=tt: e.matmul(ps[ob][:, cs], lhsT=v[:, c, :], rhs=Am[tt][1][:, cs], start=False, stop=False),
                              reads=rk, writes=[("ps", ob)])
                        P.add("pe", lambda e, ob=ob, cs=cs, c=c, gs=gs: e.matmul(ps[ob][:, cs], lhsT=Sf[:, c, :], rhs=qdT[0][:, gs], start=False, stop=False),
                              reads=rk, writes=[("ps", ob)])
                        P.add("pe", lambda e, ob=ob, cs=cs, c=c, gs=gs: e.matmul(ps[ob][:, cs], lhsT=Sbw[:, c, :], rhs=qdT[1][:, gs], start=False, stop=True),
                              reads=rk, writes=[("ps", ob)])
                    P.add("act", lambda e, sl=sl, ob=ob: e.activation(out=o32[sl][:], in_=ps[ob][:], func=AF.Copy), reads=[("ps", ob)], writes=[("o32", sl)])
                    P.add("act", lambda e, sl=sl: e.activation(out=sq[sl][:], in_=o32[sl][:], func=AF.Square), reads=[("o32", sl)], writes=[("sq", sl)])
                    P.add("pe", lambda e, sl=sl: e.matmul(ps[6][:], lhsT=ones[:], rhs=sq[sl][:], start=True, stop=True),
                          reads=["ones", ("sq", sl)], writes=[("ps", 6)])
                    P.add("dve", lambda e, sl=sl: e.tensor_scalar(out=rs[sl][:], in0=ps[6][:], scalar1=1.0 / 128.0, scalar2=float(EPS), op0=ALU.mult, op1=ALU.add),
                          reads=[("ps", 6)], writes=[("rs", sl)])
                    P.add("act", lambda e, sl=sl: e.activation(out=rs[sl][:], in_=rs[sl][:], func=AF.Sqrt), reads=[("rs", sl)], writes=[("rs", sl)])
                    P.add("dve", lambda e, sl=sl: e.reciprocal(out=rs[sl][:], in_=rs[sl][:]), reads=[("rs", sl)], writes=[("rs", sl)])
                    P.add("dve", lambda e, sl=sl: e.tensor_tensor(out=o32[sl][:], in0=o32[sl][:], in1=rs[sl][:], op=ALU.mult), reads=[("o32", sl), ("rs", sl)], writes=[("o32", sl)])
                    P.add("dve", lambda e, sl=sl, hh=hh, tsl=tsl, hs=hs: e.scalar_tensor_tensor(
                        out=ogT[hs][:, tsl], in0=o32[sl][:], scalar=hv[:, 64 + hh:65 + hh], in1=hogs[:, tsl], op0=ALU.mult, op1=ALU.mult),
                        reads=[("o32", sl), "hv", ("hogs", tt)], writes=[("ogT", hs, tt)])
                P.add("sp", lambda e, hh=hh, hs=hs: e.dma_start(out=self.ogs[:, hh, s * S:(s + 1) * S], in_=ogT[hs][:]),
                      reads=[("ogT", hs, t_) for t_ in range(4)], writes=[("ogs", hh)], dma=True, chan=("ogT", hs))

    def attn_phase(self, s):
        nc, P, ps = self.nc, self.P, self.ps
        w_in = self.w["mix_w_in"]
        GROUPS = ((1, 64, list(range(-1, 5))), (4, 256, list(range(-2, 6))), (16, 1024, list(range(-8, 12))))
        with contextlib.ExitStack() as es:
            self.uid = getattr(self, "uid", 0) + 1
            sb = lambda n, shp, dt, u=self.uid: es.enter_context(nc.sbuf_tensor("%s_u%d" % (n, u), shp, dt))
            hT = [sb("a_hT%d" % i, [128, KC, 512], BF16) for i in range(2)]
            aw2 = [[sb("a_w%d_%d" % (i, j), [128, KC, 128], BF16) for i in range(6)] for j in range(2)]
            awv = sb("a_wv", [128, KC, 384], BF16)
            qk = [sb("a_qk%d" % i, [128, S], BF16) for i in range(6)]
            vtm = sb("a_vtm", [128, 16, 384], BF16)
            rope = [sb("a_rope%d" % i, [128, S], F32) for i in range(2)]
            perm = sb("a_perm", [128, 128], BF16)
            masks = sb("a_masks", [128, 23, 512], BF16)
            onesb = sb("a_ones", [128, 128], BF16)
            pT = [sb("a_pT%d" % i, [128, 512], BF16) for i in range(4)]
            oT = [sb("a_oT%d" % i, [128, S], BF16) for i in range(2)]
            r1 = [sb("a_r1_%d" % i, [128, 512], F32) for i in range(2)]
            r2 = [sb("a_r2_%d" % i, [128, 512], F32) for i in range(2)]
            rec = [sb("a_rec%d" % i, [128, 512], F32) for i in range(2)]
            for i in range(2):
                P.add("sp", lambda e, i=i: e.dma_start(out=rope[i][:], in_=self.ropd[i]), writes=[("rope", i)], dma=True)
            P.add("pool", lambda e: e.dma_start(out=perm[:], in_=self.permd), writes=["perm"], dma=True)
            P.add("pool", lambda e: e.dma_start(out=masks[:], in_=self.amaskd.rearrange("m p q -> p m q")), writes=["masks"], dma=True)
            P.add("pool", lambda e: e.memset(onesb[:], 1.0), writes=["onesb"])
            rr = 0
            sc = 128.0 ** -0.5
            import os
            ADBG = int(os.environ.get("ADBG", "0"))
            for h in range(4 if ADBG == 0 else 1):
                hs = h % 2
                aw = aw2[hs]
                for g in range(3):
                    for wh in range(2):
                        i = g * 2 + wh
                        c0 = QKV0 + g * 1536 + wh * 512 + h * 128
                        P.add("pool", lambda e, i=i, c0=c0, aw=aw: e.dma_start(out=aw[i][:], in_=w_in[:, c0:c0 + 128].rearrange("(kc p) n -> p kc n", p=128)),
                              writes=[("aw", hs, i)], dma=True, chan=("aw", hs, i))
                    c0 = QKV0 + g * 1536 + 1024 + h * 128
                    P.add("pool", lambda e, g=g, c0=c0: e.dma_start(out=awv[:, :, g * 128:(g + 1) * 128], in_=w_in[:, c0:c0 + 128].rearrange("(kc p) n -> p kc n", p=128)),
                          writes=[("awv", g)], dma=True, chan=("awv", g))
                for tt in range(4):
                    sl = tt % 2
                    tsl = slice(tt * 512, (tt + 1) * 512)
                    P.add("sp", lambda e, sl=sl, tt=tt: e.dma_start(out=hT[sl][:], in_=self.h1Ts[:, :, s * S + tt * 512: s * S + (tt + 1) * 512]),
                          writes=[("hT", sl)], dma=True, chan=("hT", sl))
                    for i in range(6):
                        for kc in range(KC):
                            P.add("pe", lambda e, i=i, kc=kc, sl=sl, aw=aw: e.matmul(ps[i][:], lhsT=aw[i][:, kc, :], rhs=hT[sl][:, kc, :], start=(kc == 0), stop=(kc == KC - 1)),
                                  reads=[("aw", hs, i), ("hT", sl)], writes=[("ps", i)])
                        P.add("act", lambda e, i=i, tsl=tsl: e.activation(out=qk[i][:, tsl], in_=ps[i][:], func=AF.Copy), reads=[("ps", i)], writes=[("qk", i, tt)])
                        if ADBG == 1:
                            continue
                        if ADBG != 5:
                            P.add("pe", lambda e, i=i, tsl=tsl: e.matmul(ps[6][:], lhsT=perm[:], rhs=qk[i][:, tsl], start=True, stop=True),
                                  reads=[("qk", i, tt), "perm"], writes=[("ps", 6)])
                        ri = i % 2
                        if ADBG == 4:
                            continue
                        P.add("act", lambda e, i=i, ri=ri: e.activation(out=r1[ri][:], in_=ps[i][:], func=AF.Copy), reads=[("ps", i)], writes=[("r1", ri)])
                        P.add("act", lambda e, ri=ri: e.activation(out=r2[ri][:], in_=ps[6][:], func=AF.Copy), reads=[("ps", 6)], writes=[("r2", ri)])
                        P.add("dve", lambda e, i=i, ri=ri, tsl=tsl: e.tensor_tensor(out=r1[ri][:], in0=r1[ri][:], in1=rope[0][:, tsl], op=ALU.mult),
                              reads=[("r1", ri), ("rope", 0)], writes=[("r1", ri)])
                        P.add("dve", lambda e, ri=ri, tsl=tsl: e.tensor_tensor(out=r2[ri][:], in0=r2[ri][:], in1=rope[1][:, tsl], op=ALU.mult),
                              reads=[("r2", ri), ("rope", 1)], writes=[("r2", ri)])
                        P.add("dve", lambda e, i=i, ri=ri, tsl=tsl: e.tensor_tensor(out=qk[i][:, tsl], in0=r1[ri][:], in1=r2[ri][:], op=ALU.add),
                              reads=[("r1", ri), ("r2", ri)], writes=[("qk", i, tt)])
                    for c4 in range(4):
                        if ADBG == 2:
                            break
                        c = tt * 4 + c4
                        for kc in range(KC):
                            P.add("pe", lambda e, kc=kc, sl=sl, c4=c4: e.matmul(ps[7][:, 0:384], lhsT=hT[sl][:, kc, c4 * 128:(c4 + 1) * 128], rhs=awv[:, kc, :],
                                                                              start=(kc == 0), stop=(kc == KC - 1)),
                                  reads=[("awv", 0), ("awv", 1), ("awv", 2), ("hT", sl)], writes=[("ps", 7)])
                        P.add("act", lambda e, c=c: e.activation(out=vtm[:, c, :], in_=ps[7][:, 0:384], func=AF.Copy), reads=[("ps", 7)], writes=[("vtm", c)])
                for qt in range(4):
                    if ADBG >= 3:
                        break
                    nb, db = (4, 5) if qt % 2 == 0 else (6, 7)
                    qsl = slice(qt * 512, (qt + 1) * 512)
                    pairs = []
                    mbase = 0
                    for g, (dil, band, rels) in enumerate(GROUPS):
                        for rel in rels:
                            kc = 4 * qt + rel
                            if g < 2:
                                mi = mbase + rels.index(rel)
                            else:
                                mi = mbase + (rel + 8 if rel <= -5 else (4 if rel <= 7 else rel - 3))
                            if 0 <= kc <= 15:
                                pairs.append((g, kc, mi))
                        mbase += len(rels) if g < 2 else 9
                    LOOK = 2
                    slots = []
                    n = len(pairs)

                    def issue_S(idx):
                        nonlocal rr
                        g, kc, mi = pairs[idx]
                        sbk = rr % 4
                        pi = rr % 4
                        rr += 1
                        slots.append((sbk, pi))
                        P.add("pe", lambda e, g=g, kc=kc, sbk=sbk, qsl=qsl: e.matmul(ps[sbk][:], lhsT=qk[2 * g + 1][:, kc * 128:(kc + 1) * 128], rhs=qk[2 * g][:, qsl],
                                                                                   start=True, stop=True),
                              reads=[("qk", 2 * g + 1, kc // 4), ("qk", 2 * g, qt)], writes=[("ps", sbk)])
                        P.add("act", lambda e, sbk=sbk, pi=pi: e.activation(out=pT[pi][:], in_=ps[sbk][:], func=AF.Exp, scale=float(sc)),
                              reads=[("ps", sbk)], writes=[("pT", pi)])
                        P.add("dve", lambda e, pi=pi, mi=mi: e.tensor_tensor(out=pT[pi][:], in0=pT[pi][:], in1=masks[:, mi, :], op=ALU.mult),
                              reads=[("pT", pi), "masks"], writes=[("pT", pi)])

                    for idx in range(min(LOOK, n)):
                        issue_S(idx)
                    for idx, (g, kc, mi) in enumerate(pairs):
                        if idx + LOOK < n:
                            issue_S(idx + LOOK)
                        sbk, pi = slots[idx]
                        P.add("pe", lambda e, g=g, kc=kc, pi=pi, nb=nb, idx=idx, n=n: e.matmul(
                            ps[nb][:], lhsT=vtm[:, kc, g * 128:(g + 1) * 128], rhs=pT[pi][:], start=(idx == 0), stop=(idx == n - 1)),
                            reads=[("vtm", kc), ("pT", pi)], writes=[("ps", nb)])
                        P.add("pe", lambda e, pi=pi, db=db, idx=idx, n=n: e.matmul(
                            ps[db][:], lhsT=onesb[:], rhs=pT[pi][:], start=(idx == 0), stop=(idx == n - 1)),
                            reads=["onesb", ("pT", pi)], writes=[("ps", db)])
                    ri = qt % 2
                    P.add("dve", lambda e, ri=ri, db=db: e.reciprocal(out=rec[ri][:], in_=ps[db][:]), reads=[("ps", db)], writes=[("rec", ri)])
                    P.add("dve", lambda e, ri=ri, nb=nb, hs=hs, qsl=qsl: e.tensor_tensor(out=oT[hs][:, qsl], in0=rec[ri][:], in1=ps[nb][:], op=ALU.mult),
                          reads=[("ps", nb), ("rec", ri)], writes=[("oT", hs, qt)])
                P.add("sp", lambda e, h=h, hs=hs: e.dma_start(out=self.ats[:, h, s * S:(s + 1) * S], in_=oT[hs][:]),
                      reads=[("oT", hs, q_) for q_ in range(4)], writes=[("ats", h)], dma=True, chan=("oT", hs))

    def mixout_phase(self, s):
        nc, P, ps = self.nc, self.P, self.ps
        w_in = self.w["mix_w_in"]
        with contextlib.ExitStack() as es:
            self.uid = getattr(self, "uid", 0) + 1
            sb = lambda n, shp, dt, u=self.uid: es.enter_context(nc.sbuf_tensor("%s_u%d" % (n, u), shp, dt))
            self.wsl = [sb("wsl%d" % i, [128, 8192], BF16) for i in range(4)]
            wbs = [sb("m_wb%d" % i, [128, 4, 512], BF16) for i in range(2)]
            hT = sb("m_hT", [128, KC, 512], BF16)
            ogT = sb("m_ogT", [128, KC, 512], BF16)
            atT = sb("m_atT", [128, 4, 512], BF16)
            comb = sb("m_comb", [128, KC, 512], BF16)
            xa = sb("m_xa", [128, 4, D], F32)
            gbc = sb("m_gbc", [128, D], F32)
            bbc = sb("m_bbc", [128, D], F32)
            sga = [sb("m_sga%d" % i, [128, 512], F32) for i in range(2)]
            sgb = [sb("m_sgb%d" % i, [128, 512], F32) for i in range(2)]
            st = sb("m_st", [128, 4, 4 * 6], F32)
            mv = sb("m_mv", [128, 4, 2], F32)
            rstd = sb("m_rstd", [128, 4], F32)
            P.add("sp", lambda e: e.dma_start(out=gbc[:], in_=self.vec["ln2_g"].partition_broadcast(128)), writes=["gbc"], dma=True)
            P.add("sp", lambda e: e.dma_start(out=bbc[:], in_=self.vec["ln2_b"].partition_broadcast(128)), writes=["bbc"], dma=True)
            for tt in range(4):
                t0 = s * S + tt * 512
                P.add("sp", lambda e, t0=t0: e.dma_start(out=hT[:], in_=self.h1Ts[:, :, t0:t0 + 512]), writes=["hT"], dma=True)
                P.add("sp", lambda e, t0=t0: e.dma_start(out=ogT[:], in_=self.ogs[:, :, t0:t0 + 512]), writes=["ogT"], dma=True)
                P.add("sp", lambda e, t0=t0: e.dma_start(out=atT[:], in_=self.ats[:, :, t0:t0 + 512]), writes=["atT"], dma=True)
                for tc in range(4):
                    P.add("sp", lambda e, t0=t0, tc=tc: e.dma_start(out=xa[:, tc, :], in_=self.h1s[t0 + tc * 128: t0 + (tc + 1) * 128, :]), writes=[("xa", tc)], dma=True)
                    P.add("act", lambda e, tc=tc: e.activation(out=xa[:, tc, :], in_=xa[:, tc, :], func=AF.Copy, scale=float(ALPHA)),
                          reads=[("xa", tc)], writes=[("xa", tc)])
                v16 = lambda t: t[:].rearrange("p (kc n) -> p kc n", kc=KC)
                for jb in range(4):
                    s1, wga = self.load_w(w_in[:, GATE0 + jb * 512: GATE0 + (jb + 1) * 512].rearrange("(kc p) n -> p kc n", p=128), v16)
                    s2, wgb = self.load_w(w_in[:, GATE0 + 2048 + jb * 512: GATE0 + 2048 + (jb + 1) * 512].rearrange("(kc p) n -> p kc n", p=128), v16)
                    s3, wa = self.load_w(self.w["w_a"][:, jb * 512:(jb + 1) * 512].rearrange("(kc p) n -> p kc n", p=128), v16)
                    wbi = jb % 2
                    P.add("pool", lambda e, wbi=wbi, jb=jb: e.dma_start(out=wbs[wbi][:], in_=self.w["w_b"][:, jb * 512:(jb + 1) * 512].rearrange("(kc p) n -> p kc n", p=128)),
                          writes=[("wb", wbi)], dma=True, chan=("wb", wbi))
                    for jj in range(4):
                        j = jb * 4 + jj
                        b0 = (j % 2) * 4
                        cs = slice(jj * 128, (jj + 1) * 128)
                        for kc in range(KC):
                            P.add("pe", lambda e, kc=kc, cs=cs, b0=b0, wga=wga: e.matmul(ps[b0][:], lhsT=wga[:, kc, cs], rhs=hT[:, kc, :], start=(kc == 0), stop=(kc == KC - 1)),
                                  reads=[("w", s1), "hT"], writes=[("ps", b0)])
                        for kc in range(KC):
                            P.add("pe", lambda e, kc=kc, cs=cs, b0=b0, wgb=wgb: e.matmul(ps[b0 + 1][:], lhsT=wgb[:, kc, cs], rhs=hT[:, kc, :], start=(kc == 0), stop=(kc == KC - 1)),
                                  reads=[("w", s2), "hT"], writes=[("ps", b0 + 1)])
                        for kc in range(KC):
                            P.add("pe", lambda e, kc=kc, cs=cs, b0=b0, wa=wa: e.matmul(ps[b0 + 2][:], lhsT=wa[:, kc, cs], rhs=ogT[:, kc, :], start=(kc == 0), stop=(kc == KC - 1)),
                                  reads=[("w", s3), "ogT"], writes=[("ps", b0 + 2)])
                        for kc in range(4):
                            P.add("pe", lambda e, kc=kc, cs=cs, b0=b0, wbi=wbi: e.matmul(ps[b0 + 3][:], lhsT=wbs[wbi][:, kc, cs], rhs=atT[:, kc, :], start=(kc == 0), stop=(kc == 3)),
                                  reads=[("wb", wbi), "atT"], writes=[("ps", b0 + 3)])
                        jp = j % 2
                        P.add("act", lambda e, b0=b0, jp=jp: e.activation(out=sga[jp][:], in_=ps[b0][:], func=AF.Sigmoid), reads=[("ps", b0)], writes=[("sga", jp)])
                        P.add("act", lambda e, b0=b0, jp=jp: e.activation(out=sgb[jp][:], in_=ps[b0 + 1][:], func=AF.Sigmoid), reads=[("ps", b0 + 1)], writes=[("sgb", jp)])
                        P.add("dve", lambda e, b0=b0, jp=jp: e.tensor_tensor(out=sga[jp][:], in0=sga[jp][:], in1=ps[b0 + 2][:], op=ALU.mult),
                              reads=[("ps", b0 + 2), ("sga", jp)], writes=[("sga", jp)])
                        P.add("dve", lambda e, b0=b0, jp=jp: e.tensor_tensor(out=sgb[jp][:], in0=sgb[jp][:], in1=ps[b0 + 3][:], op=ALU.mult),
                              reads=[("ps", b0 + 3), ("sgb", jp)], writes=[("sgb", jp)])
                        P.add("dve", lambda e, jp=jp, j=j: e.tensor_tensor(out=comb[:, j, :], in0=sga[jp][:], in1=sgb[jp][:], op=ALU.add),
                              reads=[("sga", jp), ("sgb", jp)], writes=[("comb", j)])
                for cg in range(4):
                    s4, wo = self.load_w(self.w["w_out"][:, cg * 512:(cg + 1) * 512].rearrange("(kc p) n -> p kc n", p=128), v16)
                    for tc in range(4):
                        bnk = (cg % 2) * 4 + tc
                        for kc in range(KC):
                            P.add("pe", lambda e, kc=kc, tc=tc, bnk=bnk, wo=wo: e.matmul(ps[bnk][:], lhsT=comb[:, kc, tc * 128:(tc + 1) * 128], rhs=wo[:, kc, :],
                                                                                      start=(kc == 0), stop=(kc == KC - 1)),
                                  reads=[("w", s4), ("comb", kc)], writes=[("ps", bnk)])
                        P.add("dve", lambda e, tc=tc, cg=cg, bnk=bnk: e.tensor_tensor(out=xa[:, tc, cg * 512:(cg + 1) * 512], in0=xa[:, tc, cg * 512:(cg + 1) * 512],
                                                                                     in1=ps[bnk][:], op=ALU.add),
                              reads=[("ps", bnk), ("xa", tc)], writes=[("xa", tc)])
                for tc in range(4):
                    self.layernorm(xa[:, tc, :], ("xa", tc), st[:, tc, :], mv[:, tc, :], rstd[:, tc:tc + 1], gbc, bbc, "gbc", "bbc", tc)
                    P.add("sp", lambda e, t0=t0, tc=tc: e.dma_start(out=self.h2s[t0 + tc * 128: t0 + (tc + 1) * 128, :], in_=xa[:, tc, :]),
                          reads=[("xa", tc)], writes=[("h2s", t0, tc)], dma=True, chan=("xa", tc))


def make_consts():
    c = {}
    c["ident"] = np.eye(128, dtype=np.float32)
    cm = np.ones((1, 512), np.float32)
    cm[0, ::128] = 0.0
    c["cmask"] = cm
    sidx = np.arange(128)[:, None]
    tidx = np.arange(128)[None, :]
    tri = np.stack([np.tile((sidx <= tidx).astype(np.float32), (1, 4)),
                    np.tile((sidx >= tidx).astype(np.float32), (1, 4))])
    c["tri"] = tri
    inv = (500000.0 ** (-np.arange(0, 32, 2, dtype=np.float32) / 32.0)).astype(np.float32)
    ang = np.arange(S, dtype=np.float32)[None, :] * inv[:, None]
    cos = np.cos(ang).astype(np.float32)
    sin = np.sin(ang).astype(np.float32)
    c["rope"] = np.stack([np.concatenate([cos, cos, np.ones((96, S), np.float32)], 0),
                          np.concatenate([-sin, sin, np.zeros((96, S), np.float32)], 0)]).astype(np.float32)
    perm = np.zeros((128, 128), np.float32)
    for d in range(16):
        perm[d + 16, d] = 1.0
        perm[d, d + 16] = 1.0
    c["perm"] = perm
    masks = []
    kp = np.arange(128)[:, None]
    qf = np.arange(512)[None, :]
    for (dil, band, rels) in ((1, 64, range(-1, 5)), (4, 256, range(-2, 6)), (16, 1024, list(range(-8, -4)) + [0] + list(range(8, 12)))):
        for rel in rels:
            diff = 128 * rel + kp - qf
            masks.append(((diff % dil == 0) & (np.abs(diff) <= band)).astype(np.float32))
    c["amask"] = np.stack(masks)
    return c


_NC_CACHE = {}


def kernel(**inputs):
    nseq = 16 // NCORES
    if "nc" not in _NC_CACHE:
        _NC_CACHE["nc"] = K(nseq).build()
    nc = _NC_CACHE["nc"]
    shared = dict(make_consts())
    for n in ("ffn1_w_in", "ffn1_w_out", "ffn2_w_in", "ffn2_w_out", "mix_w_in"):
        shared[n] = np.ascontiguousarray(inputs[n][0], dtype=np.float32)
    shared["w_a"] = np.ascontiguousarray(inputs["w_branch_a"][0], dtype=np.float32)
    shared["w_b"] = np.ascontiguousarray(inputs["w_branch_b"][0], dtype=np.float32)
    shared["w_out"] = np.ascontiguousarray(inputs["mix_w_out"][0], dtype=np.float32)
    for n in ("ln1_g", "ln1_b", "ln2_g", "ln2_b", "ln3_g", "ln3_b"):
        shared[n] = np.ascontiguousarray(inputs[n][0:1], dtype=np.float32)
    hv = np.concatenate([np.asarray(inputs["hgrn_lb_fwd"][0]).reshape(16, 128).T,
                         np.asarray(inputs["hgrn_lb_fwd"][1]).reshape(16, 128).T,
                         np.asarray(inputs["hgrn_lb_bwd"][0]).reshape(16, 128).T,
                         np.asarray(inputs["hgrn_lb_bwd"][1]).reshape(16, 128).T,
                         np.asarray(inputs["hgrn_norm_g"][0]).reshape(16, 128).T], axis=1)
    shared["hvec"] = np.ascontiguousarray(hv, dtype=np.float32)
    x = np.asarray(inputs["x"], dtype=np.float32)
    in_maps = []
    for c in range(NCORES):
        m = dict(shared)
        m["x"] = np.ascontiguousarray(x[c * nseq:(c + 1) * nseq].reshape(nseq * S, D))
        in_maps.append(m)
    res = run_bass_kernel_spmd(nc, in_maps, core_ids=list(range(NCORES)))
    out = np.stack([np.asarray(r["y"]).reshape(nseq, S, D) for r in res.results], axis=0).reshape(16, S, D)
    return out.astype(np.float32)
```

```python
import contextlib
import numpy as np
import concourse.bass as bass
import concourse.mybir as mybir
from concourse.bass_utils import run_bass_kernel_spmd

F32 = mybir.dt.float32
BF16 = mybir.dt.bfloat16
AF = mybir.ActivationFunctionType
ALU = mybir.AluOpType
AX = mybir.AxisListType

D = 2048
KC = 16
DFF = 5632
NFF = 44
S = 2048
NCORES = 8
ALPHA = 2.0 ** 0.25
EPS = 1e-5
QKV0 = 5 * 2048
GATE0 = QKV0 + 4608
ENGS = ("pe", "act", "dve", "pool", "sp")


class Op:
    __slots__ = ("eng", "fn", "deps", "signal", "ticket", "chan", "idx", "is_dma")

    def __init__(self, eng, fn, is_dma=False):
        self.eng = eng
        self.fn = fn
        self.deps = []
        self.signal = False
        self.ticket = None
        self.chan = None
        self.idx = None
        self.is_dma = is_dma


class Prog:
    def __init__(self, nc):
        self.nc = nc
        self.ops = {e: [] for e in ENGS}
        self.bufs = {}
        self.dma_ops = []
        self.chan_last = {}
        self.bar_deps = []
        self.bar_pending = set()
        self.all_ops = []
        self.chmap = {}

    def add(self, eng, fn, reads=(), writes=(), dma=False, chan=None):
        op = Op(eng, fn, is_dma=dma)
        if dma:
            lk = chan if chan is not None else (writes[0] if writes else reads[0])
            lk = (eng, lk)
            if lk not in self.chmap:
                self.chmap[lk] = (eng, sum(1 for k_ in self.chmap if k_[0] == eng))
            op.chan = self.chmap[lk]
        op.idx = len(self.ops[eng])
        self.all_ops.append(op)
        deps = {}

        def add_dep(d):
            if d is None or d is op:
                return
            k = id(d) if d.is_dma else d.eng
            cur = deps.get(k)
            if cur is None or (not d.is_dma and d.idx > cur.idx):
                deps[k] = d

        for key in reads:
            ent = self.bufs.get(key)
            if ent is not None:
                add_dep(ent[0])
        for key in writes:
            ent = self.bufs.get(key)
            if ent is not None:
                add_dep(ent[0])
                for r in ent[1].values():
                    add_dep(r)
        if eng in self.bar_pending:
            self.bar_pending.discard(eng)
            for d in self.bar_deps:
                add_dep(d)
        for d in deps.values():
            if (not d.is_dma) and (not dma) and d.eng == eng and eng == "pe":
                continue
            op.deps.append(d)
            d.signal = True
        for key in reads:
            ent = self.bufs.setdefault(key, [None, {}])
            ent[1][id(op) if dma else eng] = op
        for key in writes:
            self.bufs[key] = [op, {}]
        self.ops[eng].append(op)
        if dma:
            self.dma_ops.append(op)
            self.chan_last[op.chan] = op
        return op

    def barrier(self):
        deps = []
        for e in ENGS:
            for op in reversed(self.ops[e]):
                if not op.is_dma:
                    deps.append(op)
                    break
        deps.extend(self.chan_last.values())
        self.bar_deps = deps
        self.bar_pending = set(ENGS)
        self.bufs = {}
        self.chmap = {}

    def emit(self):
        nc = self.nc
        with contextlib.ExitStack() as es:
            esem = {e: es.enter_context(nc.semaphore("es_" + e)) for e in ENGS}
            chan_sems = {}
            chan_cnt = {}
            for d in self.dma_ops:
                if d.chan not in chan_sems:
                    chan_sems[d.chan] = es.enter_context(nc.semaphore("dc_%d" % len(chan_sems)))
                    chan_cnt[d.chan] = 0
            ecnt = {e: 0 for e in ENGS}
            for op in self.all_ops:
                if op.is_dma:
                    chan_cnt[op.chan] += 16
                    op.ticket = (chan_sems[op.chan], chan_cnt[op.chan])
                elif op.signal:
                    ecnt[op.eng] += 1
                    op.ticket = (esem[op.eng], ecnt[op.eng])
            finals = list(self.chan_last.values())
            block = es.enter_context(nc.Block())

            def run(e, eng):
                waited = {}
                for op in self.ops[e]:
                    for d in op.deps:
                        sem, val = d.ticket
                        k = id(sem)
                        if waited.get(k, 0) >= val:
                            continue
                        eng.wait_ge(sem, val)
                        waited[k] = val
                    ins = op.fn(eng)
                    if op.is_dma:
                        ins.then_inc(op.ticket[0], 16)
                    elif op.signal:
                        ins.then_inc(op.ticket[0], 1)
                if e == "sp":
                    for d in finals:
                        sem, val = d.ticket
                        if waited.get(id(sem), 0) >= val:
                            continue
                        eng.wait_ge(sem, val)
                        waited[id(sem)] = val

            @block.tensor
            def _(eng):
                run("pe", eng)

            @block.scalar
            def _(eng):
                run("act", eng)

            @block.vector
            def _(eng):
                run("dve", eng)

            @block.gpsimd
            def _(eng):
                run("pool", eng)

            @block.sync
            def _(eng):
                run("sp", eng)


class K:
    def __init__(self, nseq, debug=False, upto=99, stage=0):
        self.stage = stage
        self.nseq = nseq
        self.ntok = nseq * S
        self.debug = debug
        self.upto = upto
        nc = bass.Bass("TRN2", target_bir_lowering=False)
        self.nc = nc
        self.P = Prog(nc)
        NT = self.ntok
        inp = lambda n, shp: nc.dram_tensor(n, shp, F32, kind="ExternalInput").ap()
        self.x = inp("x", [NT, D])
        self.w = {
            "ffn1_w_in": inp("ffn1_w_in", [D, 2 * DFF]), "ffn1_w_out": inp("ffn1_w_out", [DFF, D]),
            "ffn2_w_in": inp("ffn2_w_in", [D, 2 * DFF]), "ffn2_w_out": inp("ffn2_w_out", [DFF, D]),
            "mix_w_in": inp("mix_w_in", [D, 18944]), "w_a": inp("w_a", [D, D]),
            "w_b": inp("w_b", [512, D]), "w_out": inp("w_out", [D, D]),
        }
        self.vec = {n: inp(n, [1, D]) for n in ("ln1_g", "ln1_b", "ln2_g", "ln2_b", "ln3_g", "ln3_b")}
        self.hvec = inp("hvec", [128, 5 * 16])
        self.identd = inp("ident", [128, 128])
        self.cmaskd = inp("cmask", [1, 512])
        self.trid = inp("tri", [2, 128, 512])
        self.ropd = inp("rope", [2, 128, S])
        self.permd = inp("perm", [128, 128])
        self.amaskd = inp("amask", [23, 128, 512])
        kind = "ExternalOutput" if (debug or stage == 1) else ("ExternalInput" if stage == 2 else "Internal")
        self.h1s = nc.dram_tensor("h1s", [NT, D], F32, kind=kind).ap()
        self.h2s = nc.dram_tensor("h2s", [NT, D], F32, kind="ExternalOutput" if debug else "Internal").ap()
        self.h1Ts = nc.dram_tensor("h1Ts", [128, KC, NT], BF16, kind=kind).ap()
        self.ogs = nc.dram_tensor("ogs", [128, KC, NT], BF16, kind=kind).ap()
        self.ats = nc.dram_tensor("ats", [128, 4, NT], BF16, kind=kind).ap()
        self.y = nc.dram_tensor("y", [NT, D], F32, kind="ExternalOutput").ap()

    def build(self):
        nc, P = self.nc, self.P
        with contextlib.ExitStack() as es:
            sb = lambda n, shp, dt: es.enter_context(nc.sbuf_tensor(n, shp, dt))
            self.ps = [es.enter_context(nc.psum_tensor("ps%d" % i, [128, 512], F32)) for i in range(8)]
            self.ident = sb("ident_sb", [128, 128], F32)
            self.identb = sb("identb_sb", [128, 128], BF16)
            P.add("sp", lambda e: e.dma_start(out=self.ident[:], in_=self.identd), writes=["ident"], dma=True)
            P.add("pool", lambda e: e.dma_start(out=self.identb[:], in_=self.identd), writes=["identb"], dma=True)
            self.wrr = 0
            P.barrier()
            for s in range(self.nseq):
                self.ffn_phase(s, 1)
                P.barrier()
                if self.upto <= 1:
                    continue
                self.hgrn_phase(s)
                P.barrier()
                if self.upto <= 2:
                    continue
                self.attn_phase(s)
                P.barrier()
                if self.upto <= 3:
                    continue
                self.mixout_phase(s)
                P.barrier()
                self.ffn_phase(s, 2)
                P.barrier()
            P.emit()
        return nc

    def wslot(self):
        i = self.wrr
        self.wrr = (self.wrr + 1) % len(self.wsl)
        return i

    def load_w(self, src_ap, view_fn, slot=None):
        P = self.P
        if slot is None:
            slot = self.wslot()
        t = self.wsl[slot]
        dst = view_fn(t)
        P.add("pool", lambda e: e.dma_start(out=dst, in_=src_ap), writes=[("w", slot)], dma=True,
              chan=("w", slot))
        return slot, dst

    def ffn_phase(self, s, which):
        nc, P, ps = self.nc, self.P, self.ps
        if which == 1:
            src, dst = self.x, self.h1s
            w_in, w_out = self.w["ffn1_w_in"], self.w["ffn1_w_out"]
            g_d, b_d = self.vec["ln1_g"], self.vec["ln1_b"]
        else:
            src, dst = self.h2s, self.y
            w_in, w_out = self.w["ffn2_w_in"], self.w["ffn2_w_out"]
            g_d, b_d = self.vec["ln3_g"], self.vec["ln3_b"]
        with contextlib.ExitStack() as es:
            self.uid = getattr(self, "uid", 0) + 1
            sb = lambda n, shp, dt, u=self.uid: es.enter_context(nc.sbuf_tensor("%s_u%d" % (n, u), shp, dt))
            self.wsl = [sb("wsl%d" % i, [128, 8192], BF16) for i in range(4)]
            xT = sb("f_xT", [128, KC, 512], BF16)
            hid = sb("f_hid", [128, NFF, 512], BF16)
            xa = sb("f_xa", [128, 4, D], F32)
            xin = sb("f_xin", [128, D], F32)
            gbc = sb("f_gbc", [128, D], F32)
            bbc = sb("f_bbc", [128, D], F32)
            sg = [sb("f_sg%d" % i, [128, 512], F32) for i in range(2)]
            st = sb("f_st", [128, 4, 4 * 6], F32)
            mv = sb("f_mv", [128, 4, 2], F32)
            rstd = sb("f_rstd", [128, 4], F32)
            hT = sb("f_hT", [128, KC, 512], BF16) if which == 1 else None
            P.add("sp", lambda e: e.dma_start(out=gbc[:], in_=g_d.partition_broadcast(128)), writes=["gbc"], dma=True)
            P.add("sp", lambda e: e.dma_start(out=bbc[:], in_=b_d.partition_broadcast(128)), writes=["bbc"], dma=True)
            xT_keys = [("xT", q, tc) for q in range(4) for tc in range(4)]

            def load_xT(tt):
                t0 = s * S + tt * 512
                for tc in range(4):
                    P.add("sp", lambda e, tc=tc: e.dma_start(out=xin[:], in_=src[t0 + tc * 128: t0 + (tc + 1) * 128, :]),
                          writes=["xin"], dma=True)
                    for q in range(4):
                        b = q
                        for j in range(4):
                            kc = q * 4 + j
                            P.add("pe", lambda e, b=b, j=j, kc=kc: e.transpose(
                                out=ps[b][:, j * 128:(j + 1) * 128], in_=xin[:, kc * 128:(kc + 1) * 128], identity=self.ident[:]),
                                reads=["xin", "ident"], writes=[("ps", b)])
                        P.add("act", lambda e, b=b, q=q, tc=tc: e.activation(
                            out=xT[:, q * 4:(q + 1) * 4, tc * 128:(tc + 1) * 128],
                            in_=ps[b][:].rearrange("p (j t) -> p j t", j=4), func=AF.Copy),
                            reads=[("ps", b)], writes=[("xT", q, tc)])

            def featmajor_out(tt):
                t0 = s * S + tt * 512
                for tc in range(4):
                    self.to_featmajor(xa[:, tc, :], ("xa", tc), hT, "hT", tc)
                P.add("sp", lambda e: e.dma_start(out=self.h1Ts[:, :, t0:t0 + 512], in_=hT[:]),
                      reads=[("hT", q, tc) for q in range(4) for tc in range(4)], writes=[("h1Ts", t0)], dma=True, chan="hT")

            load_xT(0)
            pending_fm = None
            for tt in range(4):
                t0 = s * S + tt * 512
                for jb in range(11):
                    sg_, wg = self.load_w(w_in[:, jb * 512:(jb + 1) * 512].rearrange("(kc p) n -> p kc n", p=128),
                                          lambda t: t[:].rearrange("p (kc n) -> p kc n", kc=KC))
                    su_, wu = self.load_w(w_in[:, DFF + jb * 512: DFF + (jb + 1) * 512].rearrange("(kc p) n -> p kc n", p=128),
                                          lambda t: t[:].rearrange("p (kc n) -> p kc n", kc=KC))
                    for jj in range(4):
                        j = jb * 4 + jj
                        bg, bu = (j % 2) * 2, (j % 2) * 2 + 1
                        for kc in range(KC):
                            P.add("pe", lambda e, bg=bg, wg=wg, kc=kc, jj=jj: e.matmul(
                                ps[bg][:], lhsT=wg[:, kc, jj * 128:(jj + 1) * 128], rhs=xT[:, kc, :],
                                start=(kc == 0), stop=(kc == KC - 1)),
                                reads=[("w", sg_)] + xT_keys, writes=[("ps", bg)])
                        for kc in range(KC):
                            P.add("pe", lambda e, bu=bu, wu=wu, kc=kc, jj=jj: e.matmul(
                                ps[bu][:], lhsT=wu[:, kc, jj * 128:(jj + 1) * 128], rhs=xT[:, kc, :],
                                start=(kc == 0), stop=(kc == KC - 1)),
                                reads=[("w", su_)] + xT_keys, writes=[("ps", bu)])
                        P.add("act", lambda e, bg=bg, j=j: e.activation(out=sg[j % 2][:], in_=ps[bg][:], func=AF.Silu),
                              reads=[("ps", bg)], writes=[("sg", j % 2)])
                        P.add("dve", lambda e, bu=bu, j=j: e.tensor_tensor(out=hid[:, j, :], in0=sg[j % 2][:], in1=ps[bu][:], op=ALU.mult),
                              reads=[("ps", bu), ("sg", j % 2)], writes=[("hid", j)])
                    if jb == 1 and pending_fm is not None:
                        featmajor_out(pending_fm)
                        pending_fm = None
                    if jb == 6:
                        for tc in range(4):
                            P.add("sp", lambda e, tc=tc, t0=t0: e.dma_start(out=xa[:, tc, :], in_=src[t0 + tc * 128: t0 + (tc + 1) * 128, :]),
                                  writes=[("xa", tc)], dma=True)
                            P.add("act", lambda e, tc=tc: e.activation(out=xa[:, tc, :], in_=xa[:, tc, :], func=AF.Copy, scale=float(ALPHA)),
                                  reads=[("xa", tc)], writes=[("xa", tc)])
                for cg in range(4):
                    base = 4 if cg % 2 == 0 else 0
                    for jq in range(4):
                        sw_, wo = self.load_w(
                            w_out[jq * 1408:(jq + 1) * 1408, cg * 512:(cg + 1) * 512].rearrange("(c p) n -> p c n", p=128),
                            lambda t: t[:, 0:11 * 512].rearrange("p (c n) -> p c n", c=11))
                        for c in range(11):
                            j = jq * 11 + c
                            for tc in range(4):
                                P.add("pe", lambda e, tc=tc, j=j, c=c, wo=wo, base=base: e.matmul(
                                    ps[base + tc][:], lhsT=hid[:, j, tc * 128:(tc + 1) * 128], rhs=wo[:, c, :],
                                    start=(j == 0), stop=(j == NFF - 1)),
                                    reads=[("w", sw_), ("hid", j)], writes=[("ps", base + tc)])
                    for tc in range(4):
                        P.add("dve", lambda e, tc=tc, cg=cg, base=base: e.scalar_tensor_tensor(
                            out=xa[:, tc, cg * 512:(cg + 1) * 512], in0=ps[base + tc][:], scalar=0.5,
                            in1=xa[:, tc, cg * 512:(cg + 1) * 512], op0=ALU.mult, op1=ALU.add),
                            reads=[("ps", base + tc), ("xa", tc)], writes=[("xa", tc)])
                if tt < 3:
                    load_xT(tt + 1)
                for tc in range(4):
                    self.layernorm(xa[:, tc, :], ("xa", tc), st[:, tc, :], mv[:, tc, :], rstd[:, tc:tc + 1], gbc, bbc, "gbc", "bbc", tc)
                    P.add("sp", lambda e, tc=tc, t0=t0: e.dma_start(out=dst[t0 + tc * 128: t0 + (tc + 1) * 128, :], in_=xa[:, tc, :]),
                          reads=[("xa", tc)], writes=[("dst", which, t0, tc)], dma=True, chan=("xa", tc))
                if which == 1:
                    pending_fm = tt
            if pending_fm is not None:
                featmajor_out(pending_fm)

    def layernorm(self, z, zkey, st, mv, rstd, gbc, bbc, gk, bk, idx):
        P = self.P
        for c in range(4):
            P.add("dve", lambda e, c=c: e.bn_stats(out=st[:, c * 6:(c + 1) * 6], in_=z[:, c * 512:(c + 1) * 512]),
                  reads=[zkey], writes=[("st", idx, c)])
        P.add("dve", lambda e: e.bn_aggr(out=mv, in_=st), reads=[("st", idx, c) for c in range(4)], writes=[("mv", idx)])
        P.add("dve", lambda e: e.tensor_scalar(out=rstd, in0=mv[:, 1:2], scalar1=float(EPS), scalar2=None, op0=ALU.add),
              reads=[("mv", idx)], writes=[("rstd", idx)])
        P.add("act", lambda e: e.activation(out=rstd, in_=rstd, func=AF.Sqrt), reads=[("rstd", idx)], writes=[("rstd", idx)])
        P.add("dve", lambda e: e.reciprocal(out=rstd, in_=rstd), reads=[("rstd", idx)], writes=[("rstd", idx)])
        P.add("dve", lambda e: e.tensor_scalar(out=z, in0=z, scalar1=mv[:, 0:1], scalar2=rstd,
                                               op0=ALU.subtract, op1=ALU.mult),
              reads=[zkey, ("mv", idx), ("rstd", idx)], writes=[zkey])
        P.add("dve", lambda e: e.tensor_tensor(out=z, in0=z, in1=gbc[:], op=ALU.mult), reads=[zkey, gk], writes=[zkey])
        P.add("dve", lambda e: e.tensor_tensor(out=z, in0=z, in1=bbc[:], op=ALU.add), reads=[zkey, bk], writes=[zkey])

    def to_featmajor(self, z, zkey, hT, hkey, tc):
        P, ps = self.P, self.ps
        for q in range(4):
            b = q % 4
            for j in range(4):
                kc = q * 4 + j
                P.add("pe", lambda e, b=b, j=j, kc=kc: e.transpose(
                    out=ps[b][:, j * 128:(j + 1) * 128], in_=z[:, kc * 128:(kc + 1) * 128], identity=self.ident[:]),
                    reads=[zkey, "ident"], writes=[("ps", b)])
            if q % 2 == 0:
                fn = lambda e, b=b, q=q: e.activation(out=hT[:, q * 4:(q + 1) * 4, tc * 128:(tc + 1) * 128],
                                                      in_=ps[b][:].rearrange("p (j t) -> p j t", j=4), func=AF.Copy)
                eng = "act"
            else:
                fn = lambda e, b=b, q=q: e.tensor_copy(out=hT[:, q * 4:(q + 1) * 4, tc * 128:(tc + 1) * 128],
                                                       in_=ps[b][:].rearrange("p (j t) -> p j t", j=4))
                eng = "dve"
            P.add(eng, fn, reads=[("ps", b)], writes=[(hkey, q, tc)])

    def hgrn_phase(self, s):
        nc, P, ps = self.nc, self.P, self.ps
        w_in = self.w["mix_w_in"]
        with contextlib.ExitStack() as es:
            self.uid = getattr(self, "uid", 0) + 1
            sb = lambda n, shp, dt, u=self.uid: es.enter_context(nc.sbuf_tensor("%s_u%d" % (n, u), shp, dt))
            hT = [sb("g_hT%d" % i, [128, KC, 512], BF16) for i in range(2)]
            hw = [sb("g_hw%d" % i, [128, 5, KC, 128], BF16) for i in range(1)]
            NTMP = 13
            tmp = [[sb("g_t%d_%d" % (i, j), [128, 512], F32) for j in range(NTMP)] for i in range(2)]
            qdT = [sb("g_qdT%d" % i, [128, S], BF16) for i in range(2)]
            kdT = [sb("g_kdT%d" % i, [128, S], BF16) for i in range(2)]
            keT = [sb("g_keT%d" % i, [128, S], BF16) for i in range(2)]
            vT = sb("g_vT", [128, S], BF16)
            hogs = sb("g_hog", [128, S], BF16)
            ke = [sb("g_ke%d" % i, [128, 16, 128], BF16) for i in range(2)]
            v = sb("g_v", [128, 16, 128], BF16)
            Sf = sb("g_Sf", [128, 16, 128], BF16)
            S32 = [sb("g_S32_%d" % i, [128, 128], F32) for i in range(2)]
            dec = [sb("g_dec%d" % i, [128, 16], F32) for i in range(2)]
            Am = [[sb("g_Am%d_%d" % (i, j), [128, 512], BF16) for j in range(2)] for i in range(4)]
            Sbw = sb("g_Sbw", [128, 16, 128], BF16)
            tri = [sb("g_tri%d" % i, [128, 512], BF16) for i in range(2)]
            cmask = sb("g_cmask", [128, 512], F32)
            hv = sb("g_hv", [128, 80], F32)
            lb = [sb("g_lb%d" % i, [128, 16], F32) for i in range(2)]
            oml = [sb("g_oml%d" % i, [128, 16], F32) for i in range(2)]
            ones = sb("g_ones", [128, 128], F32)
            o32 = [sb("g_o32_%d" % i, [128, 512], F32) for i in range(2)]
            sq = [sb("g_sq_%d" % i, [128, 512], F32) for i in range(2)]
            rs = [sb("g_rs_%d" % i, [128, 512], F32) for i in range(2)]
            ogT = [sb("g_ogT%d" % i, [128, S], BF16) for i in range(2)]

            P.add("sp", lambda e: e.dma_start(out=cmask[:], in_=self.cmaskd.partition_broadcast(128)), writes=["cmask"], dma=True)
            P.add("sp", lambda e: e.dma_start(out=hv[:], in_=self.hvec), writes=["hv"], dma=True)
            for i in range(2):
                P.add("pool", lambda e, i=i: e.dma_start(out=tri[i][:], in_=self.trid[i]), writes=[("tri", i)], dma=True)
                P.add("dve", lambda e, i=i: e.tensor_tensor(out=lb[i][:], in0=hv[:, 32 * i:32 * i + 16], in1=hv[:, 32 * i + 16:32 * i + 32],
                                                            op=ALU.subtract), reads=["hv"], writes=[("lb", i)])
                P.add("act", lambda e, i=i: e.activation(out=lb[i][:], in_=lb[i][:], func=AF.Sigmoid), reads=[("lb", i)], writes=[("lb", i)])
                P.add("dve", lambda e, i=i: e.tensor_scalar(out=oml[i][:], in0=lb[i][:], scalar1=-1.0, scalar2=1.0, op0=ALU.mult, op1=ALU.add),
                      reads=[("lb", i)], writes=[("oml", i)])
            P.add("pool", lambda e: e.memset(ones[:], 1.0), writes=["ones"])

            def bcl(ap2):
                return ap2.rearrange("p (c t) -> p c t", c=4)[:, :, 127:128].to_broadcast([128, 4, 128])

            def v3(ap2):
                return ap2.rearrange("p (c t) -> p c t", c=4)

            for hh in range(16):
                hs = hh % 2
                hwS = 0
                for blk in range(5):
                    c0 = blk * 2048 + hh * 128
                    P.add("pool", lambda e, blk=blk, c0=c0: e.dma_start(
                        out=hw[0][:, blk, :, :], in_=w_in[:, c0:c0 + 128].rearrange("(kc p) n -> p kc n", p=128)),
                        writes=[("hw", 0, blk)], dma=True, chan=("hw", 0, blk))
                def projA(tt, hs=hs):
                    sl = tt % 2
                    P.add("sp", lambda e, sl=sl, tt=tt: e.dma_start(out=hT[sl][:], in_=self.h1Ts[:, :, s * S + tt * 512: s * S + (tt + 1) * 512]),
                          writes=[("hT", sl)], dma=True, chan=("hT", sl))
                    for blk in range(5):
                        for kc in range(KC):
                            P.add("pe", lambda e, blk=blk, kc=kc, sl=sl: e.matmul(
                                ps[blk][:], lhsT=hw[0][:, blk, kc, :], rhs=hT[sl][:, kc, :], start=(kc == 0), stop=(kc == KC - 1)),
                                reads=[("hw", 0, blk), ("hT", sl)], writes=[("ps", blk)])

                def evacB(tt):
                    par = tt % 2
                    T = tmp[par]
                    tsl = slice(tt * 512, (tt + 1) * 512)
                    P.add("act", lambda e, T=T: e.activation(out=T[0][:], in_=ps[0][:], func=AF.Silu), reads=[("ps", 0)], writes=[("tmp", par, "q")])
                    P.add("act", lambda e, tsl=tsl: e.activation(out=hogs[:, tsl], in_=ps[4][:], func=AF.Silu), reads=[("ps", 4)], writes=[("hogs", tt)])
                    P.add("act", lambda e, T=T: e.activation(out=T[1][:], in_=ps[1][:], func=AF.Sigmoid), reads=[("ps", 1)], writes=[("tmp", par, 0, "F")])
                    P.add("act", lambda e, T=T: e.activation(out=T[7][:], in_=ps[2][:], func=AF.Sigmoid), reads=[("ps", 2)], writes=[("tmp", par, 1, "F")])
                    P.add("act", lambda e, tsl=tsl: e.activation(out=vT[:, tsl], in_=ps[3][:], func=AF.Copy), reads=[("ps", 3)], writes=[("vT", tt)])

                def chainC(tt, hh=hh):
                    par = tt % 2
                    T = tmp[par]
                    tsl = slice(tt * 512, (tt + 1) * 512)
                    lists = []
                    for d in range(2):
                        ops = []
                        o = 8 * d
                        F, LG, KK, CU, EE, AA = (T[1], T[2], T[3], T[4], T[5], T[6]) if d == 0 else (T[7], T[8], T[9], T[10], T[11], T[12])
                        kF, kLG, kKK, kCU, kEE, kAA = [("tmp", par, d, nm) for nm in ("F", "LG", "KK", "CU", "EE", "AA")]
                        kq = ("tmp", par, "q")
                        ops.append(("dve", lambda e, F=F, d=d, hh=hh: e.tensor_scalar(out=F[:], in0=F[:], scalar1=oml[d][:, hh:hh + 1], scalar2=lb[d][:, hh:hh + 1],
                                                                                   op0=ALU.mult, op1=ALU.add), [kF, ("oml", d), ("lb", d)], [kF]))
                        ops.append(("act", lambda e, LG=LG, F=F: e.activation(out=LG[:], in_=F[:], func=AF.Ln), [kF], [kLG]))
                        ops.append(("dve", lambda e, KK=KK, F=F: e.tensor_scalar(out=KK[:], in0=F[:], scalar1=-1.0, scalar2=1.0, op0=ALU.mult, op1=ALU.add), [kF], [kKK]))
                        ops.append(("dve", lambda e, CU=CU, LG=LG: e.tensor_tensor_scan(out=CU[:], data0=cmask[:], data1=LG[:], initial=0.0, op0=ALU.mult, op1=ALU.add),
                                    [kLG, "cmask"], [kCU]))
                        ops.append(("act", lambda e, CU=CU, d=d, tt=tt: e.activation(out=dec[d][:, tt * 4:(tt + 1) * 4], in_=v3(CU[:])[:, :, 127], func=AF.Exp),
                                    [kCU], [("dec", d, tt)]))
                        if d == 0:
                            QE, kQE = CU, kCU
                        else:
                            ops.append(("dve", lambda e, AA=AA, CU=CU: e.tensor_tensor(out=v3(AA[:]), in0=bcl(CU[:]), in1=v3(CU[:]), op=ALU.subtract), [kCU], [kAA]))
                            ops.append(("dve", lambda e, AA=AA, LG=LG: e.tensor_tensor(out=AA[:], in0=AA[:], in1=LG[:], op=ALU.add), [kAA, kLG], [kAA]))
                            QE, kQE = AA, kAA
                        ops.append(("act", lambda e, EE=EE, QE=QE: e.activation(out=EE[:], in_=QE[:], func=AF.Exp), [kQE], [kEE]))
                        ops.append(("dve", lambda e, EE=EE, d=d, tsl=tsl: e.tensor_tensor(out=qdT[d][:, tsl], in0=T[0][:], in1=EE[:], op=ALU.mult), [kq, kEE], [("qdT", d, tt)]))
                        ops.append(("act", lambda e, EE=EE, QE=QE: e.activation(out=EE[:], in_=QE[:], func=AF.Exp, scale=-1.0), [kQE], [kEE]))
                        ops.append(("dve", lambda e, EE=EE, KK=KK, d=d, tsl=tsl: e.tensor_tensor(out=kdT[d][:, tsl], in0=KK[:], in1=EE[:], op=ALU.mult), [kKK, kEE], [("kdT", d, tt)]))
                        if d == 0:
                            ops.append(("dve", lambda e, AA=AA, CU=CU: e.tensor_tensor(out=v3(AA[:]), in0=bcl(CU[:]), in1=v3(CU[:]), op=ALU.subtract), [kCU], [kAA]))
                        else:
                            ops.append(("dve", lambda e, AA=AA, CU=CU, LG=LG: e.tensor_tensor(out=AA[:], in0=CU[:], in1=LG[:], op=ALU.subtract), [kCU, kLG, kEE], [kAA]))
                        ops.append(("act", lambda e, EE=EE, AA=AA: e.activation(out=EE[:], in_=AA[:], func=AF.Exp), [kAA], [kEE]))
                        ops.append(("dve", lambda e, EE=EE, KK=KK, d=d, tsl=tsl: e.tensor_tensor(out=keT[d][:, tsl], in0=KK[:], in1=EE[:], op=ALU.mult), [kKK, kEE], [("keT", d, tt)]))
                        lists.append(ops)
                    n = max(len(l) for l in lists)
                    for i in range(n):
                        for l in lists:
                            if i < len(l):
                                eng, fn, rd, wr = l[i]
                                P.add(eng, fn, reads=rd, writes=wr)

                def transD(tt):
                    for i3, (srcT, skey, dstt, dkey) in enumerate(((keT[0], ("keT", 0, tt), ke[0], ("ke", 0, tt)),
                                                                  (keT[1], ("keT", 1, tt), ke[1], ("ke", 1, tt)),
                                                                  (vT, ("vT", tt), v, ("v", tt)))):
                        bnk = 5 + (i3 % 2)
                        pb = ps[bnk][:].bitcast(BF16)
                        for c4 in range(4):
                            c = tt * 4 + c4
                            P.add("pe", lambda e, pb=pb, c4=c4, c=c, srcT=srcT: e.transpose(
                                out=pb[:, c4 * 128:(c4 + 1) * 128], in_=srcT[:, c * 128:(c + 1) * 128], identity=self.identb[:]),
                                reads=[skey, "identb"], writes=[("ps", bnk)])
                        P.add("act", lambda e, pb=pb, dstt=dstt, tt=tt: e.activation(out=dstt[:, tt * 4:(tt + 1) * 4, :],
                                                                                   in_=pb[:, 0:512].rearrange("p (c t) -> p c t", c=4), func=AF.Copy),
                              reads=[("ps", bnk)], writes=[dkey])

                projA(0); evacB(0); projA(1)
                chainC(0); transD(0); evacB(1); projA(2)
                chainC(1); transD(1); evacB(2); projA(3)
                chainC(2); transD(2); evacB(3)
                chainC(3); transD(3)
                for tt in range(4):
                    for d in range(2):
                        bnk = (tt % 2) * 2 + d
                        for c4 in range(4):
                            c = tt * 4 + c4
                            P.add("pe", lambda e, d=d, c=c, c4=c4, bnk=bnk: e.matmul(
                                ps[bnk][:, c4 * 128:(c4 + 1) * 128], lhsT=kdT[d][:, c * 128:(c + 1) * 128], rhs=qdT[d][:, c * 128:(c + 1) * 128],
                                start=True, stop=True), reads=[("kdT", d, tt), ("qdT", d, tt)], writes=[("ps", bnk)])
                        P.add("dve", lambda e, d=d, tt=tt, bnk=bnk: e.tensor_tensor(out=Am[tt][d][:], in0=ps[bnk][:], in1=tri[d][:], op=ALU.mult),
                              reads=[("ps", bnk), ("tri", d)], writes=[("Am", tt, d)])
                P.add("dve", lambda e: e.memset(S32[0][:], 0.0), writes=[("S32", 0)])
                P.add("dve", lambda e: e.memset(S32[1][:], 0.0), writes=[("S32", 1)])
                P.add("dve", lambda e: e.memset(Sf[:, 0, :], 0.0), writes=[("Sf", 0)])
                P.add("dve", lambda e: e.memset(Sbw[:, 15, :], 0.0), writes=[("Sbw", 15)])
                for i in range(15):
                    for d in range(2):
                        c = i if d == 0 else 15 - i
                        r = d * 2 + (i % 2)
                        P.add("pe", lambda e, c=c, d=d, r=r: e.matmul(ps[r][:, 0:128], lhsT=ke[d][:, c, :], rhs=v[:, c, :], start=True, stop=True),
                              reads=[("ke", d, c // 4), ("v", c // 4)], writes=[("ps", r)])
                        P.add("dve", lambda e, c=c, d=d, r=r: e.scalar_tensor_tensor(out=S32[d][:], in0=S32[d][:], scalar=dec[d][:, c:c + 1], in1=ps[r][:, 0:128],
                                                                                     op0=ALU.mult, op1=ALU.add),
                              reads=[("ps", r), ("S32", d), ("dec", d, c // 4)], writes=[("S32", d)])
                        if d == 0:
                            P.add("act", lambda e, c=c: e.activation(out=Sf[:, c + 1, :], in_=S32[0][:], func=AF.Copy), reads=[("S32", 0)], writes=[("Sf", c + 1)])
                        else:
                            P.add("act", lambda e, c=c: e.activation(out=Sbw[:, c - 1, :], in_=S32[1][:], func=AF.Copy), reads=[("S32", 1)], writes=[("Sbw", c - 1)])
                for tt in range(4):
                    sl = tt % 2
                    tsl = slice(tt * 512, (tt + 1) * 512)
                    ob = 4 + sl
                    for c4 in range(4):
                        c = tt * 4 + c4
                        cs = slice(c4 * 128, (c4 + 1) * 128)
                        gs = slice(c * 128, (c + 1) * 128)
                        rk = [("v", tt), ("Am", tt, 0), ("Am", tt, 1), ("Sf", c), ("Sbw", c), ("qdT", 0, tt), ("qdT", 1, tt)]
                        P.add("pe", lambda e, ob=ob, cs=cs, c=c, tt=tt: e.matmul(ps[ob][:, cs], lhsT=v[:, c, :], rhs=Am[tt][0][:, cs], start=True, stop=False),
                              reads=rk, writes=[("ps", ob)])
                        P.add("pe", lambda e, ob=ob, cs=cs, c=c, tt=tt: e.matmul(ps[ob][:, cs], lhsT=v[:, c, :], rhs=Am[tt][1][:, cs], start=False, stop=False),
                              reads=rk, writes=[("ps", ob)])
                        P.add("pe", lambda e, ob=ob, cs=cs, c=c, gs=gs: e.matmul(ps[ob][:, cs], lhsT=Sf[:, c, :], rhs=qdT[0][:, gs], start=False, stop=False),
                              reads=rk, writes=[("ps", ob)])
                        P.add("pe", lambda e, ob=ob, cs=cs, c=c, gs=gs: e.matmul(ps[ob][:, cs], lhsT=Sbw[:, c, :], rhs=qdT[1][:, gs], start=False, stop=True),
                              reads=rk, writes=[("ps", ob)])
                    P.add("act", lambda e, sl=sl, ob=ob: e.activation(out=o32[sl][:], in_=ps[ob][:], func=AF.Copy), reads=[("ps", ob)], writes=[("o32", sl)])
                    P.add("act", lambda e, sl=sl: e.activation(out=sq[sl][:], in_=o32[sl][:], func=AF.Square), reads=[("o32", sl)], writes=[("sq", sl)])
                    P.add("pe", lambda e, sl=sl: e.matmul(ps[6][:], lhsT=ones[:], rhs=sq[sl][:], start=True, stop=True),
                          reads=["ones", ("sq", sl)], writes=[("ps", 6)])
                    P.add("dve", lambda e, sl=sl: e.tensor_scalar(out=rs[sl][:], in0=ps[6][:], scalar1=1.0 / 128.0, scalar2=float(EPS), op0=ALU.mult, op1=ALU.add),
                          reads=[("ps", 6)], writes=[("rs", sl)])
                    P.add("act", lambda e, sl=sl: e.activation(out=rs[sl][:], in_=rs[sl][:], func=AF.Sqrt), reads=[("rs", sl)], writes=[("rs", sl)])
                    P.add("dve", lambda e, sl=sl: e.reciprocal(out=rs[sl][:], in_=rs[sl][:]), reads=[("rs", sl)], writes=[("rs", sl)])
                    P.add("dve", lambda e, sl=sl: e.tensor_tensor(out=o32[sl][:], in0=o32[sl][:], in1=rs[sl][:], op=ALU.mult), reads=[("o32", sl), ("rs", sl)], writes=[("o32", sl)])
                    P.add("dve", lambda e, sl=sl, hh=hh, tsl=tsl, hs=hs: e.scalar_tensor_tensor(
                        out=ogT[hs][:, tsl], in0=o32[sl][:], scalar=hv[:, 64 + hh:65 + hh], in1=hogs[:, tsl], op0=ALU.mult, op1=ALU.mult),
                        reads=[("o32", sl), "hv", ("hogs", tt)], writes=[("ogT", hs, tt)])
                P.add("sp", lambda e, hh=hh, hs=hs: e.dma_start(out=self.ogs[:, hh, s * S:(s + 1) * S], in_=ogT[hs][:]),
                      reads=[("ogT", hs, t_) for t_ in range(4)], writes=[("ogs", hh)], dma=True, chan=("ogT", hs))

    def attn_phase(self, s):
        nc, P, ps = self.nc, self.P, self.ps
        w_in = self.w["mix_w_in"]
        GROUPS = ((1, 64, list(range(-1, 5))), (4, 256, list(range(-2, 6))), (16, 1024, list(range(-8, 12))))
        with contextlib.ExitStack() as es:
            self.uid = getattr(self, "uid", 0) + 1
            sb = lambda n, shp, dt, u=self.uid: es.enter_context(nc.sbuf_tensor("%s_u%d" % (n, u), shp, dt))
            hT = [sb("a_hT%d" % i, [128, KC, 512], BF16) for i in range(2)]
            aw2 = [[sb("a_w%d_%d" % (i, j), [128, KC, 128], BF16) for i in range(6)] for j in range(2)]
            awv = sb("a_wv", [128, KC, 384], BF16)
            qk = [sb("a_qk%d" % i, [128, S], BF16) for i in range(6)]
            vtm = sb("a_vtm", [128, 16, 384], BF16)
            rope = [sb("a_rope%d" % i, [128, S], F32) for i in range(2)]
            perm = sb("a_perm", [128, 128], BF16)
            masks = sb("a_masks", [128, 23, 512], BF16)
            onesb = sb("a_ones", [128, 128], BF16)
            pT = [sb("a_pT%d" % i, [128, 512], BF16) for i in range(4)]
            oT = [sb("a_oT%d" % i, [128, S], BF16) for i in range(2)]
            r1 = [sb("a_r1_%d" % i, [128, 512], F32) for i in range(2)]
            r2 = [sb("a_r2_%d" % i, [128, 512], F32) for i in range(2)]
            rec = [sb("a_rec%d" % i, [128, 512], F32) for i in range(2)]
            for i in range(2):
                P.add("sp", lambda e, i=i: e.dma_start(out=rope[i][:], in_=self.ropd[i]), writes=[("rope", i)], dma=True)
            P.add("pool", lambda e: e.dma_start(out=perm[:], in_=self.permd), writes=["perm"], dma=True)
            P.add("pool", lambda e: e.dma_start(out=masks[:], in_=self.amaskd.rearrange("m p q -> p m q")), writes=["masks"], dma=True)
            P.add("pool", lambda e: e.memset(onesb[:], 1.0), writes=["onesb"])
            rr = 0
            sc = 128.0 ** -0.5
            import os
            ADBG = int(os.environ.get("ADBG", "0"))
            for h in range(4 if ADBG == 0 else 1):
                hs = h % 2
                aw = aw2[hs]
                for g in range(3):
                    for wh in range(2):
                        i = g * 2 + wh
                        c0 = QKV0 + g * 1536 + wh * 512 + h * 128
                        P.add("pool", lambda e, i=i, c0=c0, aw=aw: e.dma_start(out=aw[i][:], in_=w_in[:, c0:c0 + 128].rearrange("(kc p) n -> p kc n", p=128)),
                              writes=[("aw", hs, i)], dma=True, chan=("aw", hs, i))
                    c0 = QKV0 + g * 1536 + 1024 + h * 128
                    P.add("pool", lambda e, g=g, c0=c0: e.dma_start(out=awv[:, :, g * 128:(g + 1) * 128], in_=w_in[:, c0:c0 + 128].rearrange("(kc p) n -> p kc n", p=128)),
                          writes=[("awv", g)], dma=True, chan=("awv", g))
                for tt in range(4):
                    sl = tt % 2
                    tsl = slice(tt * 512, (tt + 1) * 512)
                    P.add("sp", lambda e, sl=sl, tt=tt: e.dma_start(out=hT[sl][:], in_=self.h1Ts[:, :, s * S + tt * 512: s * S + (tt + 1) * 512]),
                          writes=[("hT", sl)], dma=True, chan=("hT", sl))
                    for i in range(6):
                        for kc in range(KC):
                            P.add("pe", lambda e, i=i, kc=kc, sl=sl, aw=aw: e.matmul(ps[i][:], lhsT=aw[i][:, kc, :], rhs=hT[sl][:, kc, :], start=(kc == 0), stop=(kc == KC - 1)),
                                  reads=[("aw", hs, i), ("hT", sl)], writes=[("ps", i)])
                        P.add("act", lambda e, i=i, tsl=tsl: e.activation(out=qk[i][:, tsl], in_=ps[i][:], func=AF.Copy), reads=[("ps", i)], writes=[("qk", i, tt)])
                        if ADBG == 1:
                            continue
                        if ADBG != 5:
                            P.add("pe", lambda e, i=i, tsl=tsl: e.matmul(ps[6][:], lhsT=perm[:], rhs=qk[i][:, tsl], start=True, stop=True),
                                  reads=[("qk", i, tt), "perm"], writes=[("ps", 6)])
                        ri = i % 2
                        if ADBG == 4:
                            continue
                        P.add("act", lambda e, i=i, ri=ri: e.activation(out=r1[ri][:], in_=ps[i][:], func=AF.Copy), reads=[("ps", i)], writes=[("r1", ri)])
                        P.add("act", lambda e, ri=ri: e.activation(out=r2[ri][:], in_=ps[6][:], func=AF.Copy), reads=[("ps", 6)], writes=[("r2", ri)])
                        P.add("dve", lambda e, i=i, ri=ri, tsl=tsl: e.tensor_tensor(out=r1[ri][:], in0=r1[ri][:], in1=rope[0][:, tsl], op=ALU.mult),
                              reads=[("r1", ri), ("rope", 0)], writes=[("r1", ri)])
                        P.add("dve", lambda e, ri=ri, tsl=tsl: e.tensor_tensor(out=r2[ri][:], in0=r2[ri][:], in1=rope[1][:, tsl], op=ALU.mult),
                              reads=[("r2", ri), ("rope", 1)], writes=[("r2", ri)])
                        P.add("dve", lambda e, i=i, ri=ri, tsl=tsl: e.tensor_tensor(out=qk[i][:, tsl], in0=r1[ri][:], in1=r2[ri][:], op=ALU.add),
                              reads=[("r1", ri), ("r2", ri)], writes=[("qk", i, tt)])
                    for c4 in range(4):
                        if ADBG == 2:
                            break
                        c = tt * 4 + c4
                        for kc in range(KC):
                            P.add("pe", lambda e, kc=kc, sl=sl, c4=c4: e.matmul(ps[7][:, 0:384], lhsT=hT[sl][:, kc, c4 * 128:(c4 + 1) * 128], rhs=awv[:, kc, :],
                                                                              start=(kc == 0), stop=(kc == KC - 1)),
                                  reads=[("awv", 0), ("awv", 1), ("awv", 2), ("hT", sl)], writes=[("ps", 7)])
                        P.add("act", lambda e, c=c: e.activation(out=vtm[:, c, :], in_=ps[7][:, 0:384], func=AF.Copy), reads=[("ps", 7)], writes=[("vtm", c)])
                for qt in range(4):
                    if ADBG >= 3:
                        break
                    nb, db = (4, 5) if qt % 2 == 0 else (6, 7)
                    qsl = slice(qt * 512, (qt + 1) * 512)
                    pairs = []
                    mbase = 0
                    for g, (dil, band, rels) in enumerate(GROUPS):
                        for rel in rels:
                            kc = 4 * qt + rel
                            if g < 2:
                                mi = mbase + rels.index(rel)
                            else:
                                mi = mbase + (rel + 8 if rel <= -5 else (4 if rel <= 7 else rel - 3))
                            if 0 <= kc <= 15:
                                pairs.append((g, kc, mi))
                        mbase += len(rels) if g < 2 else 9
                    LOOK = 2
                    slots = []
                    n = len(pairs)

                    def issue_S(idx):
                        nonlocal rr
                        g, kc, mi = pairs[idx]
                        sbk = rr % 4
                        pi = rr % 4
                        rr += 1
                        slots.append((sbk, pi))
                        P.add("pe", lambda e, g=g, kc=kc, sbk=sbk, qsl=qsl: e.matmul(ps[sbk][:], lhsT=qk[2 * g + 1][:, kc * 128:(kc + 1) * 128], rhs=qk[2 * g][:, qsl],
                                                                                   start=True, stop=True),
                              reads=[("qk", 2 * g + 1, kc // 4), ("qk", 2 * g, qt)], writes=[("ps", sbk)])
                        P.add("act", lambda e, sbk=sbk, pi=pi: e.activation(out=pT[pi][:], in_=ps[sbk][:], func=AF.Exp, scale=float(sc)),
                              reads=[("ps", sbk)], writes=[("pT", pi)])
                        P.add("dve", lambda e, pi=pi, mi=mi: e.tensor_tensor(out=pT[pi][:], in0=pT[pi][:], in1=masks[:, mi, :], op=ALU.mult),
                              reads=[("pT", pi), "masks"], writes=[("pT", pi)])

                    for idx in range(min(LOOK, n)):
                        issue_S(idx)
                    for idx, (g, kc, mi) in enumerate(pairs):
                        if idx + LOOK < n:
                            issue_S(idx + LOOK)
                        sbk, pi = slots[idx]
                        P.add("pe", lambda e, g=g, kc=kc, pi=pi, nb=nb, idx=idx, n=n: e.matmul(
                            ps[nb][:], lhsT=vtm[:, kc, g * 128:(g + 1) * 128], rhs=pT[pi][:], start=(idx == 0), stop=(idx == n - 1)),
                            reads=[("vtm", kc), ("pT", pi)], writes=[("ps", nb)])
                        P.add("pe", lambda e, pi=pi, db=db, idx=idx, n=n: e.matmul(
                            ps[db][:], lhsT=onesb[:], rhs=pT[pi][:], start=(idx == 0), stop=(idx == n - 1)),
                            reads=["onesb", ("pT", pi)], writes=[("ps", db)])
                    ri = qt % 2
                    P.add("dve", lambda e, ri=ri, db=db: e.reciprocal(out=rec[ri][:], in_=ps[db][:]), reads=[("ps", db)], writes=[("rec", ri)])
                    P.add("dve", lambda e, ri=ri, nb=nb, hs=hs, qsl=qsl: e.tensor_tensor(out=oT[hs][:, qsl], in0=rec[ri][:], in1=ps[nb][:], op=ALU.mult),
                          reads=[("ps", nb), ("rec", ri)], writes=[("oT", hs, qt)])
                P.add("sp", lambda e, h=h, hs=hs: e.dma_start(out=self.ats[:, h, s * S:(s + 1) * S], in_=oT[hs][:]),
                      reads=[("oT", hs, q_) for q_ in range(4)], writes=[("ats", h)], dma=True, chan=("oT", hs))

    def mixout_phase(self, s):
        nc, P, ps = self.nc, self.P, self.ps
        w_in = self.w["mix_w_in"]
        with contextlib.ExitStack() as es:
            self.uid = getattr(self, "uid", 0) + 1
            sb = lambda n, shp, dt, u=self.uid: es.enter_context(nc.sbuf_tensor("%s_u%d" % (n, u), shp, dt))
            self.wsl = [sb("wsl%d" % i, [128, 8192], BF16) for i in range(4)]
            wbs = [sb("m_wb%d" % i, [128, 4, 512], BF16) for i in range(2)]
            hT = sb("m_hT", [128, KC, 512], BF16)
            ogT = sb("m_ogT", [128, KC, 512], BF16)
            atT = sb("m_atT", [128, 4, 512], BF16)
            comb = sb("m_comb", [128, KC, 512], BF16)
            xa = sb("m_xa", [128, 4, D], F32)
            gbc = sb("m_gbc", [128, D], F32)
            bbc = sb("m_bbc", [128, D], F32)
            sga = [sb("m_sga%d" % i, [128, 512], F32) for i in range(2)]
            sgb = [sb("m_sgb%d" % i, [128, 512], F32) for i in range(2)]
            st = sb("m_st", [128, 4, 4 * 6], F32)
            mv = sb("m_mv", [128, 4, 2], F32)
            rstd = sb("m_rstd", [128, 4], F32)
            P.add("sp", lambda e: e.dma_start(out=gbc[:], in_=self.vec["ln2_g"].partition_broadcast(128)), writes=["gbc"], dma=True)
            P.add("sp", lambda e: e.dma_start(out=bbc[:], in_=self.vec["ln2_b"].partition_broadcast(128)), writes=["bbc"], dma=True)
            for tt in range(4):
                t0 = s * S + tt * 512
                P.add("sp", lambda e, t0=t0: e.dma_start(out=hT[:], in_=self.h1Ts[:, :, t0:t0 + 512]), writes=["hT"], dma=True)
                P.add("sp", lambda e, t0=t0: e.dma_start(out=ogT[:], in_=self.ogs[:, :, t0:t0 + 512]), writes=["ogT"], dma=True)
                P.add("sp", lambda e, t0=t0: e.dma_start(out=atT[:], in_=self.ats[:, :, t0:t0 + 512]), writes=["atT"], dma=True)
                for tc in range(4):
                    P.add("sp", lambda e, t0=t0, tc=tc: e.dma_start(out=xa[:, tc, :], in_=self.h1s[t0 + tc * 128: t0 + (tc + 1) * 128, :]), writes=[("xa", tc)], dma=True)
                    P.add("act", lambda e, tc=tc: e.activation(out=xa[:, tc, :], in_=xa[:, tc, :], func=AF.Copy, scale=float(ALPHA)),
                          reads=[("xa", tc)], writes=[("xa", tc)])
                v16 = lambda t: t[:].rearrange("p (kc n) -> p kc n", kc=KC)
                for jb in range(4):
                    s1, wga = self.load_w(w_in[:, GATE0 + jb * 512: GATE0 + (jb + 1) * 512].rearrange("(kc p) n -> p kc n", p=128), v16)
                    s2, wgb = self.load_w(w_in[:, GATE0 + 2048 + jb * 512: GATE0 + 2048 + (jb + 1) * 512].rearrange("(kc p) n -> p kc n", p=128), v16)
                    s3, wa = self.load_w(self.w["w_a"][:, jb * 512:(jb + 1) * 512].rearrange("(kc p) n -> p kc n", p=128), v16)
                    wbi = jb % 2
                    P.add("pool", lambda e, wbi=wbi, jb=jb: e.dma_start(out=wbs[wbi][:], in_=self.w["w_b"][:, jb * 512:(jb + 1) * 512].rearrange("(kc p) n -> p kc n", p=128)),
                          writes=[("wb", wbi)], dma=True, chan=("wb", wbi))
                    for jj in range(4):
                        j = jb * 4 + jj
                        b0 = (j % 2) * 4
                        cs = slice(jj * 128, (jj + 1) * 128)
                        for kc in range(KC):
                            P.add("pe", lambda e, kc=kc, cs=cs, b0=b0, wga=wga: e.matmul(ps[b0][:], lhsT=wga[:, kc, cs], rhs=hT[:, kc, :], start=(kc == 0), stop=(kc == KC - 1)),
                                  reads=[("w", s1), "hT"], writes=[("ps", b0)])
                        for kc in range(KC):
                            P.add("pe", lambda e, kc=kc, cs=cs, b0=b0, wgb=wgb: e.matmul(ps[b0 + 1][:], lhsT=wgb[:, kc, cs], rhs=hT[:, kc, :], start=(kc == 0), stop=(kc == KC - 1)),
                                  reads=[("w", s2), "hT"], writes=[("ps", b0 + 1)])
                        for kc in range(KC):
                            P.add("pe", lambda e, kc=kc, cs=cs, b0=b0, wa=wa: e.matmul(ps[b0 + 2][:], lhsT=wa[:, kc, cs], rhs=ogT[:, kc, :], start=(kc == 0), stop=(kc == KC - 1)),
                                  reads=[("w", s3), "ogT"], writes=[("ps", b0 + 2)])
                        for kc in range(4):
                            P.add("pe", lambda e, kc=kc, cs=cs, b0=b0, wbi=wbi: e.matmul(ps[b0 + 3][:], lhsT=wbs[wbi][:, kc, cs], rhs=atT[:, kc, :], start=(kc == 0), stop=(kc == 3)),
                                  reads=[("wb", wbi), "atT"], writes=[("ps", b0 + 3)])
                        jp = j % 2
                        P.add("act", lambda e, b0=b0, jp=jp: e.activation(out=sga[jp][:], in_=ps[b0][:], func=AF.Sigmoid), reads=[("ps", b0)], writes=[("sga", jp)])
                        P.add("act", lambda e, b0=b0, jp=jp: e.activation(out=sgb[jp][:], in_=ps[b0 + 1][:], func=AF.Sigmoid), reads=[("ps", b0 + 1)], writes=[("sgb", jp)])
                        P.add("dve", lambda e, b0=b0, jp=jp: e.tensor_tensor(out=sga[jp][:], in0=sga[jp][:], in1=ps[b0 + 2][:], op=ALU.mult),
                              reads=[("ps", b0 + 2), ("sga", jp)], writes=[("sga", jp)])
                        P.add("dve", lambda e, b0=b0, jp=jp: e.tensor_tensor(out=sgb[jp][:], in0=sgb[jp][:], in1=ps[b0 + 3][:], op=ALU.mult),
                              reads=[("ps", b0 + 3), ("sgb", jp)], writes=[("sgb", jp)])
                        P.add("dve", lambda e, jp=jp, j=j: e.tensor_tensor(out=comb[:, j, :], in0=sga[jp][:], in1=sgb[jp][:], op=ALU.add),
                              reads=[("sga", jp), ("sgb", jp)], writes=[("comb", j)])
                for cg in range(4):
                    s4, wo = self.load_w(self.w["w_out"][:, cg * 512:(cg + 1) * 512].rearrange("(kc p) n -> p kc n", p=128), v16)
                    for tc in range(4):
                        bnk = (cg % 2) * 4 + tc
                        for kc in range(KC):
                            P.add("pe", lambda e, kc=kc, tc=tc, bnk=bnk, wo=wo: e.matmul(ps[bnk][:], lhsT=comb[:, kc, tc * 128:(tc + 1) * 128], rhs=wo[:, kc, :],
                                                                                      start=(kc == 0), stop=(kc == KC - 1)),
                                  reads=[("w", s4), ("comb", kc)], writes=[("ps", bnk)])
                        P.add("dve", lambda e, tc=tc, cg=cg, bnk=bnk: e.tensor_tensor(out=xa[:, tc, cg * 512:(cg + 1) * 512], in0=xa[:, tc, cg * 512:(cg + 1) * 512],
                                                                                     in1=ps[bnk][:], op=ALU.add),
                              reads=[("ps", bnk), ("xa", tc)], writes=[("xa", tc)])
                for tc in range(4):
                    self.layernorm(xa[:, tc, :], ("xa", tc), st[:, tc, :], mv[:, tc, :], rstd[:, tc:tc + 1], gbc, bbc, "gbc", "bbc", tc)
                    P.add("sp", lambda e, t0=t0, tc=tc: e.dma_start(out=self.h2s[t0 + tc * 128: t0 + (tc + 1) * 128, :], in_=xa[:, tc, :]),
                          reads=[("xa", tc)], writes=[("h2s", t0, tc)], dma=True, chan=("xa", tc))


def make_consts():
    c = {}
    c["ident"] = np.eye(128, dtype=np.float32)
    cm = np.ones((1, 512), np.float32)
    cm[0, ::128] = 0.0
    c["cmask"] = cm
    sidx = np.arange(128)[:, None]
    tidx = np.arange(128)[None, :]
    tri = np.stack([np.tile((sidx <= tidx).astype(np.float32), (1, 4)),
                    np.tile((sidx >= tidx).astype(np.float32), (1, 4))])
    c["tri"] = tri
    inv = (500000.0 ** (-np.arange(0, 32, 2, dtype=np.float32) / 32.0)).astype(np.float32)
    ang = np.arange(S, dtype=np.float32)[None, :] * inv[:, None]
    cos = np.cos(ang).astype(np.float32)
    sin = np.sin(ang).astype(np.float32)
    c["rope"] = np.stack([np.concatenate([cos, cos, np.ones((96, S), np.float32)], 0),
                          np.concatenate([-sin, sin, np.zeros((96, S), np.float32)], 0)]).astype(np.float32)
    perm = np.zeros((128, 128), np.float32)
    for d in range(16):
        perm[d + 16, d] = 1.0
        perm[d, d + 16] = 1.0
    c["perm"] = perm
    masks = []
    kp = np.arange(128)[:, None]
    qf = np.arange(512)[None, :]
    for (dil, band, rels) in ((1, 64, range(-1, 5)), (4, 256, range(-2, 6)), (16, 1024, list(range(-8, -4)) + [0] + list(range(8, 12)))):
        for rel in rels:
            diff = 128 * rel + kp - qf
            masks.append(((diff % dil == 0) & (np.abs(diff) <= band)).astype(np.float32))
    c["amask"] = np.stack(masks)
    return c


_NC_CACHE = {}


def kernel(**inputs):
    nseq = 16 // NCORES
    if "nc" not in _NC_CACHE:
        _NC_CACHE["nc"] = K(nseq).build()
    nc = _NC_CACHE["nc"]
    shared = dict(make_consts())
    for n in ("ffn1_w_in", "ffn1_w_out", "ffn2_w_in", "ffn2_w_out", "mix_w_in"):
        shared[n] = np.ascontiguousarray(inputs[n][0], dtype=np.float32)
    shared["w_a"] = np.ascontiguousarray(inputs["w_branch_a"][0], dtype=np.float32)
    shared["w_b"] = np.ascontiguousarray(inputs["w_branch_b"][0], dtype=np.float32)
    shared["w_out"] = np.ascontiguousarray(inputs["mix_w_out"][0], dtype=np.float32)
    for n in ("ln1_g", "ln1_b", "ln2_g", "ln2_b", "ln3_g", "ln3_b"):
        shared[n] = np.ascontiguousarray(inputs[n][0:1], dtype=np.float32)
    hv = np.concatenate([np.asarray(inputs["hgrn_lb_fwd"][0]).reshape(16, 128).T,
                         np.asarray(inputs["hgrn_lb_fwd"][1]).reshape(16, 128).T,
                         np.asarray(inputs["hgrn_lb_bwd"][0]).reshape(16, 128).T,
                         np.asarray(inputs["hgrn_lb_bwd"][1]).reshape(16, 128).T,
                         np.asarray(inputs["hgrn_norm_g"][0]).reshape(16, 128).T], axis=1)
    shared["hvec"] = np.ascontiguousarray(hv, dtype=np.float32)
    x = np.asarray(inputs["x"], dtype=np.float32)
    in_maps = []
    for c in range(NCORES):
        m = dict(shared)
        m["x"] = np.ascontiguousarray(x[c * nseq:(c + 1) * nseq].reshape(nseq * S, D))
        in_maps.append(m)
    res = run_bass_kernel_spmd(nc, in_maps, core_ids=list(range(NCORES)))
    out = np.stack([np.asarray(r["y"]).reshape(nseq, S, D) for r in res.results], axis=0).reshape(16, S, D)
    return out.astype(np.float32)
```
